# Optimizing a Trainium2 kernel written in Bass

```python
import jax, jax.numpy as jnp
from jax import lax
import numpy as np

D_MODEL = 1024
BATCH = 8
SEQ = 2048
DEPTH = 1

CHUNK = 64
D_CONV = 512
CONV_WIDTH = 31
N_HEADS = 8
HEAD_DIM = 64
D_ATTN = N_HEADS * HEAD_DIM
D_MIX = D_CONV + D_ATTN
Q_BLOCK = 128
D_FF = 2816
EPS = 1e-6
N_IN = 2 * D_CONV + 3 * D_ATTN + N_HEADS

kernel_name = "hybrid_conv_fox_macaron_block"


def rms_norm(x, g):
    xf = x.astype(jnp.float32)
    y = xf * lax.rsqrt(jnp.mean(xf * xf, axis=-1, keepdims=True) + EPS)
    return (y * g.astype(jnp.float32)).astype(x.dtype)


def layer_norm(x, g, b):
    xf = x.astype(jnp.float32)
    mu = jnp.mean(xf, axis=-1, keepdims=True)
    xc = xf - mu
    y = xc * lax.rsqrt(jnp.mean(xc * xc, axis=-1, keepdims=True) + EPS)
    return (y * g.astype(jnp.float32) + b.astype(jnp.float32)).astype(x.dtype)


def swiglu_ffn(h, w13, w2):
    gate, up = jnp.split(h @ w13, 2, axis=-1)
    return (jax.nn.silu(gate) * up) @ w2


def conv_module(a, g, conv_w, conv_b, ln_g, ln_b):
    u = a * jax.nn.sigmoid(g)
    u_pad = jnp.pad(u, ((0, 0), (CONV_WIDTH - 1, 0), (0, 0)))
    y = lax.conv_general_dilated(
        u_pad, conv_w[:, None, :].astype(u.dtype), window_strides=(1,), padding="VALID",
        dimension_numbers=("NWC", "WIO", "NWC"), feature_group_count=D_CONV)
    y = y + conv_b.astype(y.dtype)
    y = layer_norm(y, ln_g, ln_b)
    return jax.nn.silu(y)


def forgetting_attention(q, k, v, f_logit):
    seq = q.shape[1]
    log_f = jax.nn.log_sigmoid(f_logit.astype(jnp.float32))
    cum = jnp.cumsum(log_f, axis=1).transpose(0, 2, 1)
    scale = HEAD_DIM ** -0.5
    outs = []
    for i in range(seq // Q_BLOCK):
        q0, q1 = i * Q_BLOCK, (i + 1) * Q_BLOCK
        qb, kb, vb = q[:, q0:q1], k[:, :q1], v[:, :q1]
        s = jnp.einsum("bqhd,bkhd->bhqk", qb, kb, preferred_element_type=jnp.float32) * scale
        s = s + cum[:, :, q0:q1, None] - cum[:, :, None, :q1]
        qpos = jnp.arange(q0, q1)
        kpos = jnp.arange(q1)
        s = jnp.where(kpos[None, :] <= qpos[:, None], s, -jnp.inf)
        p = jax.nn.softmax(s, axis=-1)
        outs.append(jnp.einsum("bhqk,bkhd->bqhd", p.astype(vb.dtype), vb))
    return jnp.concatenate(outs, axis=1)


def setup_inputs(seed: int = 0) -> dict:
    key = jax.random.key(seed)
    ks = jax.random.split(key, 24)
    f32 = jnp.float32

    def nrm(k, shape, scale):
        return jax.random.normal(k, shape, f32) * scale

    def gain(k, shape):
        return 1.0 + 0.05 * jax.random.normal(k, shape, f32)

    L = DEPTH
    return {
        "x": jax.random.normal(ks[0], (BATCH, SEQ, D_MODEL), f32),
        "ffn1_norm": gain(ks[1], (L, D_MODEL)),
        "ffn1_w13": nrm(ks[2], (L, D_MODEL, 2 * D_FF), D_MODEL ** -0.5),
        "ffn1_w2": nrm(ks[3], (L, D_FF, D_MODEL), D_FF ** -0.5),
        "mix_norm": gain(ks[4], (L, D_MODEL)),
        "w_in": nrm(ks[5], (L, D_MODEL, N_IN), D_MODEL ** -0.5),
        "conv_w": nrm(ks[6], (L, CONV_WIDTH, D_CONV), CONV_WIDTH ** -0.5),
        "conv_b": nrm(ks[7], (L, D_CONV), 0.02),
        "conv_ln_g": gain(ks[8], (L, D_CONV)),
        "conv_ln_b": nrm(ks[9], (L, D_CONV), 0.02),
        "forget_b": jax.random.uniform(ks[10], (L, N_HEADS), f32, minval=1.0, maxval=4.0),
        "out_norm_conv": gain(ks[11], (L, D_CONV)),
        "out_norm_attn": gain(ks[12], (L, D_ATTN)),
        "w_out": nrm(ks[13], (L, D_MIX, D_MODEL), D_MIX ** -0.5),
        "ffn2_norm": gain(ks[14], (L, D_MODEL)),
        "ffn2_w13": nrm(ks[15], (L, D_MODEL, 2 * D_FF), D_MODEL ** -0.5),
        "ffn2_w2": nrm(ks[16], (L, D_FF, D_MODEL), D_FF ** -0.5),
        "final_norm": gain(ks[17], (D_MODEL,)),
    }


def reference(x, ffn1_norm, ffn1_w13, ffn1_w2, mix_norm, w_in, conv_w, conv_b, conv_ln_g,
              conv_ln_b, forget_b, out_norm_conv, out_norm_attn, w_out, ffn2_norm, ffn2_w13,
              ffn2_w2, final_norm):
    bsz, seq, _ = x.shape
    splits = [D_CONV, 2 * D_CONV, 2 * D_CONV + D_ATTN, 2 * D_CONV + 2 * D_ATTN,
              2 * D_CONV + 3 * D_ATTN]
    for l in range(DEPTH):
        x = x + 0.5 * swiglu_ffn(rms_norm(x, ffn1_norm[l]), ffn1_w13[l], ffn1_w2[l])

        h = rms_norm(x, mix_norm[l])
        proj = h @ w_in[l]
        a, g, q, k, v, fl = jnp.split(proj, splits, axis=-1)

        y_conv = conv_module(a, g, conv_w[l], conv_b[l], conv_ln_g[l], conv_ln_b[l])

        heads = (bsz, seq, N_HEADS, HEAD_DIM)
        y_attn = forgetting_attention(q.reshape(heads), k.reshape(heads), v.reshape(heads),
                                      fl + forget_b[l].astype(fl.dtype))
        y_attn = y_attn.reshape(bsz, seq, D_ATTN)

        y = jnp.concatenate([rms_norm(y_conv, out_norm_conv[l]),
                             rms_norm(y_attn, out_norm_attn[l])], axis=-1)
        x = x + y @ w_out[l]

        x = x + 0.5 * swiglu_ffn(rms_norm(x, ffn2_norm[l]), ffn2_w13[l], ffn2_w2[l])
    return rms_norm(x, final_norm)
```

```python
import numpy as np
from contextlib import ExitStack
import concourse.bass as bass
import concourse.mybir as mybir
from concourse.bass_utils import run_bass_kernel_spmd

F32 = mybir.dt.float32
BF16 = mybir.dt.bfloat16
AF = mybir.ActivationFunctionType
ALU = mybir.AluOpType

ENGS = ("pe", "act", "dve", "pool", "sp")

S = 2048
D = 1024
NT = 4
TS = 512
DFF = 2816
NF = 22
CW = 31
NEG = -30000.0

PC_G1, PC_GM, PC_G2, PC_GF = 0, 8, 16, 24
PC_CB, PC_LG, PC_LB, PC_ONC, PC_ONA = 32, 36, 40, 44, 48
PC_CW = 52
PC_FB = 176
NPRM = 192


class Prog:
    def __init__(self, nc, stack):
        self.nc = nc
        self.stack = stack
        self.ops = {e: [] for e in ENGS}
        self.cnt = {}
        self.sems = {}
        self.lastw = {}
        self.readers = {}
        self.base = ()
        for e in ENGS:
            self._sem(e)

    def _sem(self, key):
        if key not in self.sems:
            self.sems[key] = self.stack.enter_context(self.nc.semaphore("s%d" % len(self.sems)))
            self.cnt[key] = 0
        return self.sems[key]

    def barrier(self):
        self.base = tuple((k, v) for k, v in self.cnt.items() if v > 0)

    def _deps(self, reads, writes, extra):
        deps = set(self.base)
        for r in reads:
            t = self.lastw.get(r)
            if t is not None:
                deps.add(t)
        for w in writes:
            t = self.lastw.get(w)
            if t is not None:
                deps.add(t)
            for t in self.readers.get(w, ()):
                deps.add(t)
        for t in extra:
            if t is not None:
                deps.add(t)
        return deps

    def _commit(self, tok, reads, writes):
        for r in reads:
            self.readers.setdefault(r, []).append(tok)
        for w in writes:
            self.lastw[w] = tok
            self.readers[w] = []

    def op(self, eng, fn, reads=(), writes=(), extra=()):
        deps = self._deps(reads, writes, extra)
        self.cnt[eng] += 1
        tok = (eng, self.cnt[eng])
        self.ops[eng].append((deps, fn, tok, 1))
        self._commit(tok, reads, writes)
        return tok

    def dma(self, eng, fn, semkey, reads=(), writes=(), extra=()):
        self._sem(semkey)
        deps = self._deps(reads, writes, extra)
        self.cnt[semkey] += 16
        tok = (semkey, self.cnt[semkey])
        self.ops[eng].append((deps, fn, tok, 16))
        self._commit(tok, reads, writes)
        return tok

    def run(self, final_tokens):
        nc = self.nc
        engmap = {"pe": "tensor", "act": "scalar", "dve": "vector", "pool": "gpsimd", "sp": "sync"}
        with nc.Block() as block:
            for e in ENGS:
                ops = self.ops[e]

                def body(engine, ops=ops, e=e):
                    waited = {}
                    for deps, fn, tok, inc in ops:
                        best = {}
                        for (k, v) in deps:
                            if best.get(k, 0) < v:
                                best[k] = v
                        for k, v in best.items():
                            if k == e and e == "pe":
                                continue
                            if waited.get(k, 0) < v:
                                engine.wait_ge(self.sems[k], v)
                                waited[k] = v
                        ins = fn(engine)
                        ins.then_inc(self.sems[tok[0]], inc)
                    if e == "sp":
                        for (k, v) in final_tokens:
                            engine.wait_ge(self.sems[k], v)

                getattr(block, engmap[e])(body)


def build_program(stop_after=None):
    nc = bass.Bass("TRN2", target_bir_lowering=False)

    def din(name, shape):
        return nc.dram_tensor(name, shape, F32, kind="ExternalInput").ap()

    xT = din("xT", [D, S])
    prm = din("prm", [128, NPRM])
    w13d = [din("w13a", [NF, 128, 2048]), din("w13b", [NF, 128, 2048])]
    w2d = [din("w2a", [NF, 128, 1024]), din("w2b", [NF, 128, 1024])]
    wind = din("win", [16, 128, 1024])
    wvd = din("wv", [2, 128, 2048])
    wfd = din("wf", [128, 64])
    wod = din("wo", [8, 128, 1024])
    outT = nc.dram_tensor("outT", [D, S], F32, kind="ExternalOutput").ap()
    xTv = xT.rearrange("(c p) t -> p c t", p=128)
    outTv = outT.rearrange("(c p) t -> p c t", p=128)

    with ExitStack() as st:
        P = Prog(nc, st)
        PSB = [st.enter_context(nc.psum_tensor("psb%d" % i, [128, 2, 512], F32)) for i in range(4)]

        class _Bank:
            def __init__(self, b):
                self.b = b

            def __getitem__(self, key):
                r, c_ = key
                return PSB[self.b // 2][r, self.b % 2, c_]
        PS = [_Bank(i) for i in range(8)]

        BASE = 16640
        LIMIT = 229376 - 64
        cur = [BASE]

        def T(name, shape, dt, at=None):
            n = 1
            for s_ in shape[1:]:
                n *= s_
            nbytes = n * (4 if dt == F32 else 2)
            nbytes = (nbytes + 63) // 64 * 64
            if at is None:
                off = cur[0]
                cur[0] += nbytes
            else:
                off = at
            assert off + nbytes <= LIMIT, (name, off, nbytes)
            return nc.alloc_sbuf_tensor_at(name, shape, dt, offset=off)

        X = T("X", [128, 8, S], F32)
        PRM = T("PRM", [128, NPRM], F32)
        GS = T("GS", [128, 48], F32)
        EPS = T("EPS", [128, 4], F32)
        ONES = T("ONES", [128, 128], BF16)
        IDENT = T("IDENT", [128, 128], BF16)
        MASK = T("MASK", [128, 128], BF16)
        ZER = T("ZER", [128, 128], BF16)
        MISC_T = T("MISC_T", [128, TS], F32)
        cur[0] = BASE + 65536 + 4096
        M0 = cur[0]
        AVAIL = LIMIT - M0

        def stage_alloc():
            pos = [M0]

            def A(name, shape, dt):
                n = 1
                for s_ in shape[1:]:
                    n *= s_
                nbytes = (n * (4 if dt == F32 else 2) + 63) // 64 * 64
                off = pos[0]
                pos[0] += nbytes
                assert pos[0] <= LIMIT, (name, pos[0] - M0, AVAIL)
                return nc.alloc_sbuf_tensor_at(name, shape, dt, offset=off)
            A.pos = pos
            return A

        uid = [0]

        def nm(s_):
            uid[0] += 1
            return "%s_%d" % (s_, uid[0])

        def tsl(t):
            return slice(t * TS, (t + 1) * TS)

        P.dma("sp", lambda e: e.dma_start(out=PRM[:], in_=prm), "d_prm", writes=["PRM"])
        for t in range(NT):
            P.dma("sp", lambda e, t=t: e.dma_start(out=X[:, :, tsl(t)], in_=xTv[:, :, tsl(t)]), ("d_x", t),
                  writes=[("X", c, t) for c in range(8)])
        P.op("pool", lambda e: e.memset(ONES[:], 1.0), writes=["ONES"])
        P.op("pool", lambda e: e.memset(ZER[:], 0.0), writes=["ZER"])
        P.op("pool", lambda e: e.affine_select(out=IDENT[:], in_=ONES[:], pattern=[[1, 128]], compare_op=ALU.is_equal,
                                               fill=0.0, base=0, channel_multiplier=-1), reads=["ONES"], writes=["IDENT"])
        P.op("pool", lambda e: e.affine_select(out=MASK[:], in_=ZER[:], pattern=[[1, 128]], compare_op=ALU.is_ge,
                                               fill=NEG, base=0, channel_multiplier=-1), reads=["ZER"], writes=["MASK"])
        P.op("pool", lambda e: e.memset(EPS[:, 0:1], 1e-6 * D), writes=["EPS0"])
        P.op("pool", lambda e: e.memset(EPS[:, 1:2], 1e-6 * 512), writes=["EPS1"])
        P.op("pool", lambda e: e.memset(EPS[:, 2:3], 1e-6), writes=["EPS2"])
        P.op("dve", lambda e: e.tensor_scalar(out=GS[:, 0:32], in0=PRM[:, 0:32], scalar1=float(np.sqrt(D)), scalar2=None,
                                              op0=ALU.mult), reads=["PRM"], writes=["GS0"])
        P.op("dve", lambda e: e.tensor_scalar(out=GS[:, 32:40], in0=PRM[:, PC_ONC:PC_ONC + 8], scalar1=float(np.sqrt(512.0)),
                                              scalar2=None, op0=ALU.mult), reads=["PRM"], writes=["GS1"])
        GSR = ["GS0", "GS1", "EPS0", "EPS1", "EPS2"]

        rot = {}

        def rotate(key, lst):
            i = rot.get(key, 0)
            rot[key] = i + 1
            return lst[i % len(lst)]

        def stat_rstd(sq_fn, nch, sq_reads, bank, lnt, rs, eps_col, tag):
            def mm(e):
                ins = None
                for c in range(nch):
                    ins = e.matmul(PS[bank][:, :], ONES[:, :], sq_fn(c), start=(c == 0), stop=(c == nch - 1))
                return ins
            P.op("pe", mm, reads=list(sq_reads) + ["ONES"], writes=[("ps", bank)])
            P.op("act", lambda e: e.activation(out=lnt[:, :], in_=PS[bank][:, :], func=AF.Ln,
                                               bias=EPS[:, eps_col:eps_col + 1], scale=1.0),
                 reads=[("ps", bank)] + GSR, writes=[(tag, "lnt"), (tag, "rs")])
            P.op("act", lambda e: e.activation(out=rs[:, :], in_=lnt[:, :], func=AF.Exp, scale=-0.5),
                 reads=[(tag, "lnt")], writes=[(tag, "rs")])

        def x_norm_ops(t, gcol, SQ, LNT, RS, dst_fn, dst_region, banks):
            b = t % 2
            sq = SQ[b]
            ops = []
            for c in range(8):
                ops.append(lambda c=c: P.op("act", lambda e: e.activation(out=sq[:, c, :], in_=X[:, c, tsl(t)], func=AF.Square),
                                            reads=[("X", c, t)], writes=[("SQ", sq.name, c)]))

            def st_():
                bank = rotate("statb", banks)
                stat_rstd(lambda c: sq[:, c, :], 8, [("SQ", sq.name, c) for c in range(8)], bank, LNT[b], RS[b], 0, ("xn", b))
            ops.append(st_)
            for c in range(8):
                ops.append(lambda c=c: P.op("dve", lambda e: e.scalar_tensor_tensor(
                    out=dst_fn(c), in0=X[:, c, tsl(t)], scalar=GS[:, gcol + c:gcol + c + 1], in1=RS[b][:, :],
                    op0=ALU.mult, op1=ALU.mult),
                    reads=[("X", c, t), (("xn", b), "rs")] + GSR, writes=[dst_region(c)]))
            return ops

        def x_norm(t, gcol, SQ, LNT, RS, dst_fn, dst_region, banks):
            for f_ in x_norm_ops(t, gcol, SQ, LNT, RS, dst_fn, dst_region, banks):
                f_()

        def ffn_stage(which):
            A = stage_alloc()
            H = A(nm("H"), [128, 8, S], BF16)
            R13 = [A(nm("R13"), [128, 2048], BF16) for _ in range(8)]
            R2 = [A(nm("R2"), [128, 1024], BF16) for _ in range(9)]
            ACTB = [A(nm("AB"), [128, 6, TS], BF16) for _ in range(2)]
            SG = [A(nm("SG"), [128, TS], F32) for _ in range(2)]
            SQ = [A(nm("SQ"), [128, 8, TS], BF16) for _ in range(2)]
            LNT = [A(nm("LNT"), [128, TS], F32) for _ in range(2)]
            RS = [A(nm("RS"), [128, TS], F32) for _ in range(2)]
            gcol = PC_G1 if which == 0 else PC_G2
            w13 = w13d[which]
            w2 = w2d[which]
            sfx = "f%d" % which
            def RG(*a):
                return (sfx,) + a

            groups = [list(range(0, 6)), list(range(6, 12)), list(range(12, 17)), list(range(17, 22))]
            st13 = {"next": 0, "slot_of": {}, "occ": [None] * 8, "done": set()}
            st2 = {"next": 0, "slot_of": {}, "occ": [None] * 9, "done": set()}

            rot13 = 4 if which == 1 else 0
            pstate = {"extra": (), "limit": NF}

            def pump():
                progressed = True
                while progressed:
                    progressed = False
                    for stt, ring, wsrc, key, nslot in ((st13, R13, w13, "d13", 8), (st2, R2, w2, "d2", 9)):
                        fi = stt["next"]
                        if fi >= NF or fi >= pstate["limit"]:
                            continue
                        if stt is st2 and st13["next"] <= fi and st13["next"] < NF:
                            continue
                        s_ = (fi + (rot13 if stt is st13 else 0)) % nslot
                        prev = stt["occ"][s_]
                        if prev is not None and prev not in stt["done"]:
                            continue
                        P.dma("pool", lambda e, s_=s_, fi=fi, ring=ring, wsrc=wsrc: e.dma_start(out=ring[s_][:, :], in_=wsrc[fi]),
                              (key, s_), writes=[RG(key, s_)], extra=pstate["extra"])
                        stt["occ"][s_] = fi
                        stt["slot_of"][fi] = s_
                        stt["next"] = fi + 1
                        progressed = True

            if which == 1:
                pstate["extra"] = (hbox["tok_lastpv"],)
                pstate["limit"] = 4
                pump()
                pstate["extra"] = ()
                pstate["limit"] = NF
            P.barrier()

            normq = []

            def drip(n):
                for _ in range(n):
                    if not normq:
                        return
                    normq.pop(0)()

            def hnorm(t, gc):
                normq.extend(x_norm_ops(t, gc, SQ, LNT, RS, lambda c, t=t: H[:, c, tsl(t)], lambda c, t=t: RG("H", c, t), [6, 7]))

            def tail_ops(t):
                if which == 0:
                    return x_norm_ops(t, PC_GM, SQ, LNT, RS, lambda c, t=t: H[:, c, tsl(t)], lambda c, t=t: RG("H", c, t), [6, 7])
                ops_ = x_norm_ops(t, PC_GF, SQ, LNT, RS, lambda c, t=t: X[:, c, tsl(t)], lambda c, t=t: ("X", c, t), [6, 7])
                if t < NT - 1:
                    ops_.append(lambda t=t: P.dma("sp", lambda e: e.dma_start(out=outTv[:, :, tsl(t)], in_=X[:, :, tsl(t)]),
                                                  ("d_out", t % 2), reads=[("X", c, t) for c in range(8)]))
                    return ops_
                res = ops_[:9]
                for c_ in range(8):
                    res.append(ops_[9 + c_])
                    res.append(lambda c_=c_: P.dma("sp", lambda e: e.dma_start(out=outTv[:, c_, tsl(t)], in_=X[:, c_, tsl(t)]),
                                                   ("d_out", t % 2), reads=[("X", c_, t)]))
                return res
            hnorm(0, gcol)
            drip(1000)
            pump()

            ucount = [0]

            def U(g, t, i, fi):
                n = ucount[0]
                ucount[0] += 1
                b = n % 2
                s13 = st13["slot_of"][fi]
                pg, pu = 0 + b, 2 + b
                ab = t % 2

                def mm(e):
                    ins = None
                    for kc in range(8):
                        ins = e.matmul(PS[pg][:, :], R13[s13][:, kc * 256:kc * 256 + 128], H[:, kc, tsl(t)],
                                       start=(kc == 0), stop=(kc == 7))
                    for kc in range(8):
                        ins = e.matmul(PS[pu][:, :], R13[s13][:, kc * 256 + 128:kc * 256 + 256], H[:, kc, tsl(t)],
                                       start=(kc == 0), stop=(kc == 7))
                    return ins
                P.op("pe", mm, reads=[RG("d13", s13)] + [RG("H", c, t) for c in range(8)], writes=[("ps", pg), ("ps", pu)])
                P.op("act", lambda e: e.activation(out=SG[b][:, :], in_=PS[pg][:, :], func=AF.Silu),
                     reads=[("ps", pg)], writes=[RG("SG", b)])
                P.op("dve", lambda e: e.tensor_tensor(out=ACTB[ab][:, i, :], in0=SG[b][:, :], in1=PS[pu][:, :], op=ALU.mult),
                     reads=[RG("SG", b), ("ps", pu)], writes=[RG("A", ab, i)])
                if t == NT - 1:
                    st13["done"].add(fi)

            def O(g, t, grp):
                ab = t % 2
                tops = None
                if g == len(groups) - 1:
                    if t >= 2:
                        drip(1000)
                    tops = tail_ops(t)
                for j in range(8):
                    po = rotate("po", [4, 5])

                    def mm(e, j=j, po=po):
                        ins = None
                        for i, fi in enumerate(grp):
                            s2 = st2["slot_of"][fi]
                            ins = e.matmul(PS[po][:, :], R2[s2][:, j * 128:(j + 1) * 128], ACTB[ab][:, i, :],
                                           start=(i == 0), stop=(i == len(grp) - 1))
                        return ins
                    P.op("pe", mm, reads=[RG("d2", st2["slot_of"][fi]) for fi in grp] + [RG("A", ab, i) for i in range(len(grp))],
                         writes=[("ps", po)])
                    P.op("dve", lambda e, j=j, po=po: e.scalar_tensor_tensor(out=X[:, j, tsl(t)], in0=PS[po][:, :], scalar=0.5,
                                                                              in1=X[:, j, tsl(t)], op0=ALU.mult, op1=ALU.add),
                         reads=[("ps", po), ("X", j, t)], writes=[("X", j, t)])
                    if tops is not None:
                        tops[j]()
                if t == NT - 1:
                    for fi in grp:
                        st2["done"].add(fi)
                if tops is not None:
                    normq.extend(tops[8:])

            pending = None
            for g, grp in enumerate(groups):
                for t in range(NT):
                    for i, fi in enumerate(grp):
                        pump()
                        if g == 0 and i == 0:
                            drip(1000)
                        U(g, t, i, fi)
                        drip(4)
                        if g == 0 and i == 0 and t + 1 < NT:
                            hnorm(t + 1, gcol)
                        if pending is not None:
                            O(*pending)
                            pending = None
                        pump()
                    pending = (g, t, grp)
            if pending is not None:
                O(*pending)
            drip(1000)
            assert st13["next"] == NF and st2["next"] == NF
            return H

        def mixer_stage(H):
            P.barrier()
            A = stage_alloc()
            A.pos[0] = M0 + 32768
            YNC = A("mYNC", [128, 4, S], BF16)
            mark = A.pos[0]
            U_ = A("mU", [128, 4, 32 + S], BF16)
            DG = A("mDG", [128, 4 * CW, 128], BF16)
            p_wb = A.pos[0]
            WB = [A("mWB%d" % i, [128, 1024], BF16) for i in range(4)]
            p_sq = A.pos[0]
            SQ = [A("mSQ%d" % i, [128, 8, TS], BF16) for i in range(1)]
            LNT = [A("mLNT%d" % i, [128, TS], F32) for i in range(2)]
            RS = [A("mRS%d" % i, [128, TS], F32) for i in range(2)]
            p_sig = A.pos[0]
            SIG = [A("mSIG%d" % i, [128, TS], F32) for i in range(2)]
            YB = A("mYB", [128, 4, TS], BF16)
            YSQ = A("mYSQ", [128, 4, TS], BF16)
            SW = A("mSW", [128, 4, TS], F32)
            YCT = nc.alloc_sbuf_tensor_at("mYCT", [128, 4, TS], F32, offset=p_sq)
            MU = nc.alloc_sbuf_tensor_at("mMU", [128, TS], F32, offset=p_sig)
            VAR = nc.alloc_sbuf_tensor_at("mVAR", [128, TS], F32, offset=p_sig + 2048)
            PADC = 2

            def RG(*a):
                return ("mx",) + a


            wst = {"n": 0, "occ": [None] * 4, "done": set(), "slot": {}}

            def load_win(ci, WBl, keyp, extra=()):
                s_ = wst["n"] % 4
                wst["n"] += 1
                P.dma("pool", lambda e: e.dma_start(out=WBl[s_][:, :], in_=wind[ci]), (keyp, s_), writes=[RG(keyp, s_)],
                      extra=extra)
                wst["slot"][ci] = s_
                return s_

            P.op("pool", lambda e: e.memset(U_[:, :, 0:32], 0.0), writes=[RG("Upad")])
            dg_list = [(c, j) for c in range(4) for j in range(CW)]

            def dg_emit(n):
                for _ in range(n):
                    if not dg_list:
                        return
                    c, j = dg_list.pop(0)
                    if j % 2 == 0:
                        P.op("act", lambda e, c=c, j=j: e.activation(out=DG[:, c * CW + j, :], in_=IDENT[:, :], func=AF.Copy,
                                                                     scale=PRM[:, PC_CW + c * CW + j:PC_CW + c * CW + j + 1]),
                             reads=["IDENT", "PRM"], writes=[RG("DG", c, j)])
                    else:
                        P.op("dve", lambda e, c=c, j=j: e.tensor_scalar(out=DG[:, c * CW + j, :], in0=IDENT[:, :],
                                                                        scalar1=PRM[:, PC_CW + c * CW + j:PC_CW + c * CW + j + 1],
                                                                        scalar2=None, op0=ALU.mult),
                             reads=["IDENT", "PRM"], writes=[RG("DG", c, j)])

            def proj_chunk(ci, t, bank, WBl, keyp):
                s_ = wst["slot"][ci]

                def mm(e):
                    ins = None
                    for kc in range(8):
                        ins = e.matmul(PS[bank][:, :], WBl[s_][:, kc * 128:(kc + 1) * 128], H[:, kc, tsl(t)],
                                       start=(kc == 0), stop=(kc == 7))
                    return ins
                P.op("pe", mm, reads=[RG(keyp, s_)] + [RG("H", c, t) for c in range(8)], writes=[("ps", bank)])

            for c in range(4):
                load_win(4 + c, WB, "dwb")
                load_win(c, WB, "dwb")
                for t in range(NT):
                    bg = rotate("pa", [0, 1, 2, 3])
                    ba = rotate("pa", [0, 1, 2, 3])
                    proj_chunk(4 + c, t, bg, WB, "dwb")
                    proj_chunk(c, t, ba, WB, "dwb")
                    sb_ = rotate("sig", [0, 1])
                    P.op("act", lambda e, bg=bg, sb_=sb_: e.activation(out=SIG[sb_][:, :], in_=PS[bg][:, :], func=AF.Sigmoid),
                         reads=[("ps", bg)], writes=[RG("SIG", sb_)])
                    P.op("dve", lambda e, ba=ba, sb_=sb_, c=c, t=t: e.tensor_tensor(
                        out=U_[:, c, PADC + 30 + t * TS:PADC + 30 + (t + 1) * TS], in0=SIG[sb_][:, :], in1=PS[ba][:, :], op=ALU.mult),
                        reads=[RG("SIG", sb_), ("ps", ba)], writes=[RG("U", c, t)])
                    dg_emit(8)
            dg_emit(1000)
            ONE64 = nc.alloc_sbuf_tensor_at("mO64", [128, 64], BF16, offset=mark + 93440)
            WF = nc.alloc_sbuf_tensor_at("mWF", [128, 64], BF16, offset=mark + 93568)
            assert mark + 93696 <= LIMIT
            P.op("pool", lambda e: e.memset(ONE64[:], 1.0), writes=[RG("O64")])
            P.dma("pool", lambda e: e.dma_start(out=WF[:, :], in_=wfd), "d_wf", writes=[RG("WF")])

            YCT2 = nc.alloc_sbuf_tensor_at("mYCT2", [128, 4, TS], F32, offset=p_wb)
            YCTS = [YCT, YCT2]

            conv_banks = {}

            def conv_pe(t):
                conv_banks[t] = []
                for c in range(4):
                    bank = rotate("pc", [0, 1, 5, 6])
                    conv_banks[t].append(bank)

                    def mm(e, c=c, t=t, bank=bank):
                        ins = None
                        for j in range(CW):
                            o = PADC + t * TS + j
                            ins = e.matmul(PS[bank][:, :], DG[:, c * CW + j, :], U_[:, c, o:o + TS],
                                           start=(j == 0), stop=(j == CW - 1))
                        return ins
                    rd = [RG("DG", c, j) for j in range(CW)] + [RG("U", c, t), RG("Upad")]
                    if t > 0:
                        rd.append(RG("U", c, t - 1))
                    P.op("pe", mm, reads=rd, writes=[("ps", bank)])

            def conv_evac(t):
                Y = YCTS[t % 2]
                for c in range(4):
                    bank = conv_banks[t][c]
                    P.op("act", lambda e, c=c, bank=bank, Y=Y: e.activation(out=Y[:, c, :], in_=PS[bank][:, :], func=AF.Identity,
                                                                            bias=PRM[:, PC_CB + c:PC_CB + c + 1], scale=1.0),
                         reads=[("ps", bank), "PRM"], writes=[RG("YCT", t % 2, c)])

            def conv_act_yb(t):
                Y = YCTS[t % 2]
                P.op("act", lambda e: e.activation(out=YB[:, :, :], in_=Y[:, :, :], func=AF.Copy),
                     reads=[RG("YCT", t % 2, c) for c in range(4)], writes=[RG("YB")])

            def conv_act_ysq(t):
                Y = YCTS[t % 2]
                P.op("act", lambda e: e.activation(out=YSQ[:, :, :], in_=Y[:, :, :], func=AF.Square),
                     reads=[RG("YCT", t % 2, c) for c in range(4)], writes=[RG("YSQ")])

            def conv_chain_act(t):
                conv_act_yb(t)
                conv_act_ysq(t)

            def conv_chain(t):
                bm, bq = 2, 3

                def mm2(e):
                    ins = None
                    for c in range(4):
                        ins = e.matmul(PS[bm][:, :], ONES[:, :], YB[:, c, :], start=(c == 0), stop=(c == 3))
                    for c in range(4):
                        ins = e.matmul(PS[bq][:, :], ONES[:, :], YSQ[:, c, :], start=(c == 0), stop=(c == 3))
                    return ins
                P.op("pe", mm2, reads=[RG("YB"), RG("YSQ"), "ONES"], writes=[("ps", bm), ("ps", bq)])

            def conv_chain_b_steps(t):
                Y = YCTS[t % 2]
                yb_ = t % 2
                bm, bq = 2, 3
                st_ = []
                if t + 1 < NT:
                    st_.append(lambda: conv_act_yb(t + 1))
                st_.append(lambda: P.op("dve", lambda e: e.tensor_scalar(out=MU[:, :], in0=PS[bm][:, :], scalar1=1.0 / 512,
                                                                         scalar2=None, op0=ALU.mult),
                                        reads=[("ps", bm)], writes=[RG("MU")]))
                st_.append(lambda: P.op("dve", lambda e: e.tensor_tensor(out=VAR[:, :], in0=MU[:, :], in1=MU[:, :], op=ALU.mult),
                                        reads=[RG("MU")], writes=[RG("VAR")]))
                st_.append(lambda: P.op("dve", lambda e: e.scalar_tensor_tensor(out=VAR[:, :], in0=PS[bq][:, :], scalar=1.0 / 512,
                                                                                in1=VAR[:, :], op0=ALU.mult, op1=ALU.subtract),
                                        reads=[("ps", bq), RG("VAR")], writes=[RG("VAR")]))
                st_.append(lambda: P.op("act", lambda e: e.activation(out=LNT[0][:, :], in_=VAR[:, :], func=AF.Ln, bias=EPS[:, 2:3],
                                                                      scale=1.0),
                                        reads=[RG("VAR")] + GSR, writes=[RG("cl"), RG("crs")]))
                st_.append(lambda: P.op("act", lambda e: e.activation(out=RS[0][:, :], in_=LNT[0][:, :], func=AF.Exp, scale=-0.5),
                                        reads=[RG("cl")], writes=[RG("crs")]))
                for c in range(4):
                    st_.append(lambda c=c: P.op("dve", lambda e: e.tensor_tensor(out=Y[:, c, :], in0=Y[:, c, :], in1=MU[:, :],
                                                                                 op=ALU.subtract),
                                                reads=[RG("YCT", yb_, c), RG("MU")], writes=[RG("YCT", yb_, c)]))
                    st_.append(lambda c=c: P.op("dve", lambda e: e.tensor_tensor(out=Y[:, c, :], in0=Y[:, c, :], in1=RS[0][:, :],
                                                                                 op=ALU.mult),
                                                reads=[RG("YCT", yb_, c), RG("crs")], writes=[RG("YCT", yb_, c)]))
                    st_.append(lambda c=c: P.op("act", lambda e: e.activation(out=SW[:, c, :], in_=Y[:, c, :], func=AF.Silu,
                                                                              bias=PRM[:, PC_LB + c:PC_LB + c + 1],
                                                                              scale=PRM[:, PC_LG + c:PC_LG + c + 1]),
                                                reads=[RG("YCT", yb_, c), "PRM"], writes=[RG("SW", c)]))
                st_.append(lambda: P.op("act", lambda e: e.activation(out=YSQ[:, :, :], in_=SW[:, :, :], func=AF.Square),
                                        reads=[RG("SW", c) for c in range(4)], writes=[RG("YSQ")]))
                def c2_pe():
                    def mm(e):
                        ins = None
                        for c in range(4):
                            ins = e.matmul(PS[4][:, :], ONES[:, :], YSQ[:, c, :], start=(c == 0), stop=(c == 3))
                        return ins
                    P.op("pe", mm, reads=[RG("YSQ"), "ONES"], writes=[("ps", 4)])

                def c2_act():
                    tag = RG("c2")
                    P.op("act", lambda e: e.activation(out=LNT[1][:, :], in_=PS[4][:, :], func=AF.Ln, bias=EPS[:, 1:2], scale=1.0),
                         reads=[("ps", 4)] + GSR, writes=[(tag, "lnt"), (tag, "rs")])
                    P.op("act", lambda e: e.activation(out=RS[1][:, :], in_=LNT[1][:, :], func=AF.Exp, scale=-0.5),
                         reads=[(tag, "lnt")], writes=[(tag, "rs")])
                st_.append(c2_pe)
                if t + 1 < NT:
                    st_.append(lambda: conv_act_ysq(t + 1))
                st_.append(c2_act)
                for c in range(4):
                    st_.append(lambda c=c: P.op("dve", lambda e: e.scalar_tensor_tensor(out=YNC[:, c, tsl(t)], in0=SW[:, c, :],
                                                                                        scalar=GS[:, 32 + c:33 + c], in1=RS[1][:, :],
                                                                                        op0=ALU.mult, op1=ALU.mult),
                                                reads=[RG("SW", c), (RG("c2"), "rs")] + GSR, writes=[RG("YNC", c, t)]))
                return st_

            def conv_chain_b(t, other=()):
                other = list(other)
                for i_, f_ in enumerate(conv_chain_b_steps(t)):
                    f_()
                    if other and i_ % 2 == 1:
                        other.pop(0)()
                while other:
                    other.pop(0)()

            QA = nc.alloc_sbuf_tensor_at("mQA", [70, 8, S], BF16, offset=mark)
            WBQ = [nc.alloc_sbuf_tensor_at("mWQ%d" % i, [128, 1024], BF16, offset=mark + 32768 + 2048 * i) for i in range(4)]
            qstate = {}

            def q_proj_steps(c):
                ci = 8 + c
                tk = qstate["tok"]
                P.dma("pool", lambda e: e.dma_start(out=WBQ[c][:, :], in_=wind[ci]), ("dwq", c), writes=[RG("dwq", c)], extra=[tk])
                return [lambda t=t: q_iter(c, t, tk) for t in range(NT)]

            def q_iter(c, t, tk):
                if True:
                    bank = rotate("pq", [7, 0, 1, 5, 6])

                    def mm(e, t=t, bank=bank):
                        ins = None
                        for kc in range(8):
                            ins = e.matmul(PS[bank][:, :], WBQ[c][:, kc * 128:(kc + 1) * 128], H[:, kc, tsl(t)],
                                           start=(kc == 0), stop=(kc == 7))
                        return ins
                    P.op("pe", mm, reads=[RG("dwq", c)] + [RG("H", c_, t) for c_ in range(8)], writes=[("ps", bank)])
                    P.op("act", lambda e, t=t, bank=bank: e.activation(out=QA[0:64, 2 * c, tsl(t)], in_=PS[bank][0:64, :],
                                                                       func=AF.Copy, scale=0.125),
                         reads=[("ps", bank)], writes=[RG("Q", 2 * c, t)], extra=[tk])
                    P.op("dve", lambda e, t=t, bank=bank: e.tensor_scalar(out=QA[0:64, 2 * c + 1, tsl(t)], in0=PS[bank][64:128, :],
                                                                          scalar1=0.125, scalar2=None, op0=ALU.mult),
                         reads=[("ps", bank)], writes=[RG("Q", 2 * c + 1, t)], extra=[tk])

            conv_pe(0)
            conv_evac(0)
            conv_chain_act(0)
            conv_pe(1)
            conv_evac(1)
            for t in range(NT):
                conv_chain(t)
                if t + 2 < NT:
                    conv_pe(t + 2)
                    if t + 2 == NT - 1:
                        qstate["tok"] = ("pe", P.cnt["pe"])
                oth = []
                if t >= 2:
                    oth = q_proj_steps(2 * (t - 2)) + q_proj_steps(2 * (t - 2) + 1)
                conv_chain_b(t, oth)
                if t + 2 < NT:
                    conv_evac(t + 2)

            P.barrier()
            A.pos[0] = mark + 32768
            KA = A("mKA", [70, 8, S], BF16)
            p_vp = A.pos[0]
            VP = A("mVP", [128, 16, 512], BF16)
            p_w = A.pos[0]
            WB2 = [A("mWC%d" % i, [128, 1024], BF16) for i in range(4)]
            p_spare = A.pos[0]
            WVB = [nc.alloc_sbuf_tensor_at("mWV%d" % i, [128, 2048], BF16, offset=p_w + 4096 * i) for i in range(2)]
            SIGF = nc.alloc_sbuf_tensor_at("mSIGF", [8, S], F32, offset=p_vp)
            DD = nc.alloc_sbuf_tensor_at("mDD", [8, S], F32, offset=p_vp + 8192)
            D1 = nc.alloc_sbuf_tensor_at("mD1", [8, S], BF16, offset=p_vp)
            D2 = nc.alloc_sbuf_tensor_at("mD2", [8, S], BF16, offset=p_vp + 4096)


            for t in range(NT):
                bank = rotate("pa", [0, 1, 2, 3])

                def mm(e, t=t, bank=bank):
                    ins = None
                    for kc in range(8):
                        ins = e.matmul(PS[bank][0:8, :], WF[:, kc * 8:(kc + 1) * 8], H[:, kc, tsl(t)], start=(kc == 0), stop=(kc == 7))
                    return ins
                P.op("pe", mm, reads=[RG("WF")] + [RG("H", c, t) for c in range(8)], writes=[("ps", bank)])
                P.op("act", lambda e, t=t, bank=bank: e.activation(out=SIGF[0:8, tsl(t)], in_=PS[bank][0:8, :], func=AF.Sigmoid,
                                                                   bias=PRM[0:8, PC_FB:PC_FB + 1], scale=1.0),
                     reads=[("ps", bank), "PRM"], writes=[RG("SIGF", t)])
            P.op("act", lambda e: e.activation(out=SIGF[0:8, :], in_=SIGF[0:8, :], func=AF.Ln),
                 reads=[RG("SIGF", t) for t in range(NT)], writes=[RG("LOGF")])
            dtoks = []
            pieces = []
            for h_ in range(8):
                for t_ in range(NT):
                    pieces.append(lambda h_=h_, t_=t_: P.op("dve", lambda e: e.memset(QA[64:70, h_, tsl(t_)], -1.0),
                                                            writes=[RG("QAaug")]))
                    pieces.append(lambda h_=h_, t_=t_: P.op("pool", lambda e: e.memset(KA[64:70, h_, tsl(t_)], 1.0),
                                                            writes=[RG("KAaug")]))
            dq = [
                lambda: P.op("dve", lambda e: e.tensor_tensor_scan(out=DD[0:8, :], data0=SIGF[0:8, :], data1=SIGF[0:8, :],
                                                                   initial=0.0, op0=ALU.add, op1=ALU.min),
                             reads=[RG("LOGF")], writes=[RG("DD")]),
                lambda: P.op("dve", lambda e: e.tensor_copy(out=D1[0:8, :], in_=DD[0:8, :]), reads=[RG("DD")], writes=[RG("D1")]),
                lambda: P.op("dve", lambda e: e.tensor_tensor(out=DD[0:8, :], in0=DD[0:8, :], in1=D1[0:8, :], op=ALU.subtract),
                             reads=[RG("DD"), RG("D1")], writes=[RG("DD")]),
                lambda: P.op("dve", lambda e: e.tensor_copy(out=D2[0:8, :], in_=DD[0:8, :]), reads=[RG("DD")], writes=[RG("D2")]),
                lambda: P.op("dve", lambda e: e.tensor_tensor(out=DD[0:8, :], in0=DD[0:8, :], in1=D2[0:8, :], op=ALU.subtract),
                             reads=[RG("DD"), RG("D2")], writes=[RG("DD")]),
            ]

            def d12_dmas():
                while pieces:
                    pieces.pop(0)()
                for i, Di in enumerate((D1, D2)):
                    dtoks.append(P.dma("sp", lambda e, i=i, Di=Di: e.dma_start(out=QA[64 + i:65 + i, :, :], in_=Di[0:8, :]),
                                       ("d_dq", i), reads=[RG("D%d" % (i + 1)), RG("QAaug")], writes=[RG("QAd", i)]))
                    dtoks.append(P.dma("sp", lambda e, i=i, Di=Di: e.dma_start(out=KA[67 + i:68 + i, :, :], in_=Di[0:8, :]),
                                       ("d_dk", i), reads=[RG("D%d" % (i + 1)), RG("KAaug")], writes=[RG("KAd", i)]))
            dq.append(d12_dmas)
            AUGR = [RG("QAd", i) for i in range(3)] + [RG("KAd", i) for i in range(3)]

            def d3_dmas():
                dtoks.append(P.dma("pool", lambda e: e.dma_start(out=QA[66:67, :, :], in_=DD[0:8, :]), ("d_dq", 2),
                                   reads=[RG("DD"), RG("QAaug")], writes=[RG("QAd", 2)]))
                dtoks.append(P.dma("pool", lambda e: e.dma_start(out=KA[69:70, :, :], in_=DD[0:8, :]), ("d_dk", 2),
                                   reads=[RG("DD"), RG("KAaug")], writes=[RG("KAd", 2)]))

            wst["n"] = 0
            wst["slot"] = {}
            ktok = {}
            for qk in range(1, 2):
                for c in range(4):
                    ci = 8 + qk * 4 + c
                    load_win(ci, WB2, "dwc")
                    if c == 3:
                        P.dma("pool", lambda e: e.dma_start(out=WVB[0][:, :], in_=wvd[0]), ("d_wv", 0), writes=[RG("WV", 0)],
                              extra=[ktok[1]])
                    for t in range(NT):
                        bank = rotate("pa", [0, 1, 2, 3])
                        proj_chunk(ci, t, bank, WB2, "dwc")
                        dst = QA if qk == 0 else KA
                        scl = 0.125 if qk == 0 else 1.0
                        nmk = "Q" if qk == 0 else "K"
                        P.op("act", lambda e, dst=dst, c=c, t=t, bank=bank, scl=scl: e.activation(
                            out=dst[0:64, 2 * c, tsl(t)], in_=PS[bank][0:64, :], func=AF.Copy, scale=scl),
                            reads=[("ps", bank)], writes=[RG(nmk, 2 * c, t)])
                        P.op("dve", lambda e, dst=dst, c=c, t=t, bank=bank, scl=scl: e.tensor_scalar(
                            out=dst[0:64, 2 * c + 1, tsl(t)], in0=PS[bank][64:128, :], scalar1=scl, scalar2=None, op0=ALU.mult),
                            reads=[("ps", bank)], writes=[RG(nmk, 2 * c + 1, t)])
                        if t == NT - 1:
                            ktok[c] = ("pe", P.cnt["pe"])
                        for _ in range(4):
                            if pieces:
                                pieces.pop(0)()
                        if dq and (t % 2 == 1):
                            dq.pop(0)()
            while dq:
                dq.pop(0)()
            P.dma("pool", lambda e: e.dma_start(out=WVB[1][:, :], in_=wvd[1]), ("d_wv", 1), writes=[RG("WV", 1)],
                  extra=[ktok[3]])
            d3_dmas()
            for hf in range(2):
                for blk in range(16):
                    bank = rotate("pa", [0, 1, 2, 3])
                    t = blk // 4

                    def mm(e, blk=blk, bank=bank, hf=hf):
                        ins = None
                        for kc in range(8):
                            ins = e.matmul(PS[bank][:, 0:256], H[:, kc, blk * 128:(blk + 1) * 128],
                                           WVB[hf][:, kc * 256:(kc + 1) * 256], start=(kc == 0), stop=(kc == 7))
                        return ins
                    P.op("pe", mm, reads=[RG("WV", hf)] + [RG("H", c, t) for c in range(8)], writes=[("ps", bank)])
                    if blk % 2 == 0:
                        P.op("act", lambda e, blk=blk, bank=bank, hf=hf: e.activation(out=VP[:, blk, hf * 256:(hf + 1) * 256],
                                                                                      in_=PS[bank][:, 0:256], func=AF.Copy),
                             reads=[("ps", bank)], writes=[RG("VPh", blk, hf)], extra=dtoks)
                    else:
                        P.op("dve", lambda e, blk=blk, bank=bank, hf=hf: e.tensor_copy(out=VP[:, blk, hf * 256:(hf + 1) * 256],
                                                                                       in_=PS[bank][:, 0:256]),
                             reads=[("ps", bank)], writes=[RG("VPh", blk, hf)], extra=dtoks)
            tok_lastH = ("pe", P.cnt["pe"])

            A.pos[0] = M0
            WO = [A("mWO%d" % j, [128, 1024], BF16) for j in range(8)]
            YAT = A("mYAT", [128, 4, TS], F32)
            YNA = A("mYNA", [128, 4, TS], BF16)
            SSQ = A("mSSQ", [128, 4, TS], BF16)
            assert A.pos[0] <= M0 + 32768, A.pos[0] - M0
            A.pos[0] = p_w
            PT = [A("mPT%d" % i, [128, 2, TS], BF16) for i in range(3)]
            assert A.pos[0] <= p_w + 8192
            A.pos[0] = p_spare
            RR = A("mRR", [128, TS], F32)
            LN3 = MISC_T
            RS3 = MISC_T
            for j in range(8):
                P.dma("pool", lambda e, j=j: e.dma_start(out=WO[j][:, :], in_=wod[j]), ("d_wo", j), writes=[RG("WO", j)],
                      extra=[tok_lastH])

            pvbanks = {}
            for I in range(NT):
                for c in range(4):
                    pvbanks[(I, c)] = rotate("pvb", [(4, 5), (6, 7)])

            def unit(I, c, j):
                q0 = I * TS
                nkb = 4 * I + 4
                d = j - 4 * I
                n0 = 128 * d if d > 0 else 0
                k = rotate("sbp", [0, 1])
                sbanks = (2 * k, 2 * k + 1)
                pb = rotate("ptb", [0, 1, 2])
                bA, bB = pvbanks[(I, c)]
                last = (j == nkb - 1)

                def qk(e):
                    ins = None
                    for hh in range(2):
                        h = 2 * c + hh
                        sbk = sbanks[hh]
                        e.matmul(PS[sbk][0:64, n0:TS], KA[0:70, h, j * 128:j * 128 + 64], QA[0:70, h, q0 + n0:q0 + TS],
                                 start=True, stop=(d < 0))
                        ins = e.matmul(PS[sbk][64:128, n0:TS], KA[0:70, h, j * 128 + 64:(j + 1) * 128],
                                       QA[0:70, h, q0 + n0:q0 + TS], start=True, stop=(d < 0), tile_position=(0, 64))
                        if d >= 0:
                            e.matmul(PS[sbk][0:64, n0:n0 + 128], IDENT[:, 0:64], MASK[:, :], start=False, stop=True)
                            ins = e.matmul(PS[sbk][64:128, n0:n0 + 128], IDENT[:, 64:128], MASK[:, :], start=False, stop=True,
                                           tile_position=(0, 64))
                    return ins
                rd = ["IDENT", "MASK"] + AUGR
                for hh in range(2):
                    rd += [RG("K", 2 * c + hh, j // 4), RG("Q", 2 * c + hh, I)]
                P.op("pe", qk, reads=rd, writes=[("ps", sbanks[0]), ("ps", sbanks[1])])
                P.op("act", lambda e: e.activation(out=PT[pb][:, :, n0:TS], in_=PSB[k][:, :, n0:TS], func=AF.Exp),
                     reads=[("ps", sbanks[0]), ("ps", sbanks[1])], writes=[RG("PT", pb)])

                def pv():
                    def mm(e):
                        lo, hi = (slice(0, 64), slice(64, 128))
                        vA = VP[:, j, (2 * c) * 64:(2 * c + 1) * 64]
                        vB = VP[:, j, (2 * c + 1) * 64:(2 * c + 2) * 64]
                        e.matmul(PS[bA][lo, n0:TS], vA, PT[pb][:, 0, n0:TS], start=(j == 0), stop=last)
                        e.matmul(PS[bB][hi, n0:TS], ONE64[:, :], PT[pb][:, 0, n0:TS], start=(j == 0), stop=last,
                                 tile_position=(0, 64))
                        e.matmul(PS[bB][lo, n0:TS], ONE64[:, :], PT[pb][:, 1, n0:TS], start=(j == 0), stop=last)
                        return e.matmul(PS[bA][hi, n0:TS], vB, PT[pb][:, 1, n0:TS], start=(j == 0), stop=last,
                                        tile_position=(0, 64))
                    P.op("pe", mm, reads=[RG("PT", pb), RG("VPh", j, 0), RG("VPh", j, 1), RG("O64")], writes=[("ps", bA), ("ps", bB)])
                return pv

            def norm_pair(I, c):
                bA, bB = pvbanks[(I, c)]
                P.op("dve", lambda e: e.reciprocal(out=RR[:, :], in_=PS[bB][:, :]), reads=[("ps", bB)], writes=[RG("RR", 0), RG("RR", 1)])
                P.op("dve", lambda e: e.tensor_tensor(out=YAT[0:64, c, :], in0=PS[bA][0:64, :], in1=RR[64:128, :], op=ALU.mult),
                     reads=[("ps", bA), RG("RR", 1)], writes=[RG("YAT", c, 0)])
                P.op("dve", lambda e: e.tensor_tensor(out=YAT[64:128, c, :], in0=PS[bA][64:128, :], in1=RR[0:64, :], op=ALU.mult),
                     reads=[("ps", bA), RG("RR", 0)], writes=[RG("YAT", c, 1)])

            yr = [RG("YAT", c, k_) for c in range(4) for k_ in range(2)]

            def ssq(I):
                P.op("dve", lambda e: e.tensor_tensor(out=SSQ[:, :, :], in0=YAT[:, :, :], in1=YAT[:, :, :], op=ALU.mult),
                     reads=yr, writes=[RG("SSQ")])

            def epi2(I):
                bst = rotate("sb", [0, 1, 2, 3])
                stat_rstd(lambda c: SSQ[:, c, :], 4, [RG("SSQ")], bst, LN3, RS3, 1, RG("a3"))
                for c in range(4):
                    P.op("dve", lambda e, c=c: e.scalar_tensor_tensor(out=YNA[:, c, :], in0=YAT[:, c, :],
                                                                     scalar=GS[:, 36 + c:37 + c], in1=RS3[:, :],
                                                                     op0=ALU.mult, op1=ALU.mult),
                         reads=[RG("YAT", c, 0), RG("YAT", c, 1), (RG("a3"), "rs")] + GSR, writes=[RG("YNA", c)])
                if stop_after == "mixer_y":
                    P.dma("pool", lambda e: e.dma_start(out=outTv[:, 4:8, tsl(I)], in_=YNA[:, :, :]), ("d_out", 2),
                          reads=[RG("YNA", c) for c in range(4)])
                    P.dma("pool", lambda e: e.dma_start(out=outTv[:, 0:4, tsl(I)], in_=YNC[:, :, tsl(I)]), ("d_out", 3),
                          reads=[RG("YNC", c, I) for c in range(4)])

            def wout(I):
                for jo in range(8):
                    bo = rotate("sb", [0, 1, 2, 3])

                    def mm(e, jo=jo, bo=bo):
                        ins = None
                        for kc in range(8):
                            rhs = YNC[:, kc, tsl(I)] if kc < 4 else YNA[:, kc - 4, :]
                            ins = e.matmul(PS[bo][:, :], WO[jo][:, kc * 128:(kc + 1) * 128], rhs, start=(kc == 0), stop=(kc == 7))
                        return ins
                    P.op("pe", mm, reads=[RG("WO", jo)] + [RG("YNC", c, I) for c in range(4)] + [RG("YNA", c) for c in range(4)],
                         writes=[("ps", bo)])
                    P.op("dve", lambda e, jo=jo, bo=bo: e.tensor_tensor(out=X[:, jo, tsl(I)], in0=PS[bo][:, :], in1=X[:, jo, tsl(I)],
                                                                      op=ALU.add),
                         reads=[("ps", bo), ("X", jo, I)], writes=[("X", jo, I)])

            items = []
            for I in range(NT):
                for c in range(4):
                    for j in range(4 * I + 4):
                        items.append(("unit", I, c, j))
                    hooks = []
                    if c == 0 and I > 0:
                        hooks.append(lambda I=I: epi2(I - 1))
                    hooks.append(lambda I=I, c=c: norm_pair(I, c))
                    if c == 3:
                        hooks.append(lambda I=I: ssq(I))
                    if c == 0 and I > 0:
                        hooks.append(lambda I=I: wout(I - 1))
                    if c == 3 and I == NT - 1:
                        hooks.append(lambda I=I: epi2(I))
                        hooks.append(lambda I=I: wout(I))
                    items.append(("hooks", hooks))
            pendq = []
            for it in items:
                if it[0] == "unit":
                    pvf = unit(it[1], it[2], it[3])
                    pendq.append([pvf, []])
                    if len(pendq) > 2:
                        old = pendq.pop(0)
                        old[0]()
                        for hk in old[1]:
                            hk()
                else:
                    pendq[-1][1].extend(it[1])
            while pendq:
                old = pendq.pop(0)
                old[0]()
                if not pendq:
                    hbox["tok_lastpv"] = ("pe", P.cnt["pe"])
                for hk in old[1]:
                    hk()

        def final_stage():
            P.barrier()
            A = stage_alloc()
            SQ = [A("fSQ%d" % i, [128, 8, TS], BF16) for i in range(2)]
            LNT = [A("fLNT%d" % i, [128, TS], F32) for i in range(2)]
            RS = [A("fRS%d" % i, [128, TS], F32) for i in range(2)]
            OT = [A("fOT%d" % i, [128, 8, TS], F32) for i in range(2)]
            for t in range(NT):
                b = t % 2
                x_norm(t, PC_GF, SQ, LNT, RS, lambda c, b=b: OT[b][:, c, :], lambda c, b=b: ("OT", b, c), [6, 7])
                P.dma("sp", lambda e, t=t, b=b: e.dma_start(out=outTv[:, :, tsl(t)], in_=OT[b][:, :, :]), ("d_out", b),
                      reads=[("OT", b, c) for c in range(8)])

        def dump_stage():
            P.barrier()
            for t in range(NT):
                P.dma("sp", lambda e, t=t: e.dma_start(out=outTv[:, :, tsl(t)], in_=X[:, :, tsl(t)]), ("d_out", t % 2),
                      reads=[("X", c, t) for c in range(8)])

        hbox = {}
        stages = [("ffn1", lambda: hbox.__setitem__("H", ffn_stage(0))), ("mixer", lambda: mixer_stage(hbox["H"])),
                  ("ffn2", lambda: ffn_stage(1))]
        stopped = False
        for sname, sfn in stages:
            sfn()
            if stop_after == "mixer_y" and sname == "mixer":
                stopped = True
                break
            if stop_after == sname:
                dump_stage()
                stopped = True
                break
        if stop_after == "ffn1":
            raise RuntimeError("ffn1 dump unsupported with fused norms")
        finals = [(k, v) for k, v in P.cnt.items() if isinstance(k, tuple) and k[0] == "d_out"]
        P.run(finals)
    return nc


_CACHE = {}


def _prep_w13(w13):
    g = w13[:, :DFF].reshape(8, 128, NF, 128)
    u = w13[:, DFF:].reshape(8, 128, NF, 128)
    gu = np.concatenate([g, u], axis=3)
    return np.ascontiguousarray(gu.transpose(2, 1, 0, 3)).reshape(NF, 128, 2048)


def _prep_cols(w, ncol):
    a = w.reshape(8, 128, ncol, 128)
    return np.ascontiguousarray(a.transpose(2, 1, 0, 3)).reshape(ncol, 128, 1024)


def kernel(x, ffn1_norm, ffn1_w13, ffn1_w2, mix_norm, w_in, conv_w, conv_b, conv_ln_g, conv_ln_b, forget_b,
           out_norm_conv, out_norm_attn, w_out, ffn2_norm, ffn2_w13, ffn2_w2, final_norm):
    f32 = np.float32
    x = np.asarray(x, f32)
    B = x.shape[0]

    def vec8(v):
        return np.asarray(v, f32).reshape(8, 128).T

    def vec4(v):
        return np.asarray(v, f32).reshape(4, 128).T

    prm = np.zeros((128, NPRM), f32)
    prm[:, PC_G1:PC_G1 + 8] = vec8(ffn1_norm[0])
    prm[:, PC_GM:PC_GM + 8] = vec8(mix_norm[0])
    prm[:, PC_G2:PC_G2 + 8] = vec8(ffn2_norm[0])
    prm[:, PC_GF:PC_GF + 8] = vec8(final_norm)
    prm[:, PC_CB:PC_CB + 4] = vec4(conv_b[0])
    prm[:, PC_LG:PC_LG + 4] = vec4(conv_ln_g[0])
    prm[:, PC_LB:PC_LB + 4] = vec4(conv_ln_b[0])
    prm[:, PC_ONC:PC_ONC + 4] = vec4(out_norm_conv[0])
    prm[:, PC_ONA:PC_ONA + 4] = vec4(out_norm_attn[0])
    cw = np.asarray(conv_w[0], f32)
    prm[:, PC_CW:PC_CW + 4 * CW] = cw.reshape(CW, 4, 128).transpose(2, 1, 0).reshape(128, 4 * CW)
    prm[0:8, PC_FB] = np.asarray(forget_b[0], f32)

    win = np.asarray(w_in[0], f32)
    shared = {
        "prm": prm,
        "w13a": _prep_w13(np.asarray(ffn1_w13[0], f32)),
        "w2a": np.ascontiguousarray(np.asarray(ffn1_w2[0], f32).reshape(NF, 128, 1024)),
        "w13b": _prep_w13(np.asarray(ffn2_w13[0], f32)),
        "w2b": np.ascontiguousarray(np.asarray(ffn2_w2[0], f32).reshape(NF, 128, 1024)),
        "win": _prep_cols(win[:, 0:2048], 16),
        "wv": np.ascontiguousarray(win[:, 2048:2560].reshape(8, 128, 2, 256).transpose(2, 1, 0, 3)).reshape(2, 128, 2048),
        "wf": np.ascontiguousarray(win[:, 2560:2568].reshape(8, 128, 8).transpose(1, 0, 2)).reshape(128, 64),
        "wo": _prep_cols(np.asarray(w_out[0], f32), 8),
    }
    if "nc" not in _CACHE:
        _CACHE["nc"] = build_program()
    nc = _CACHE["nc"]
    in_maps = []
    for b in range(B):
        m = dict(shared)
        m["xT"] = np.ascontiguousarray(x[b].T)
        in_maps.append(m)
    res = run_bass_kernel_spmd(nc, in_maps, core_ids=list(range(B)))
    out = np.empty((B, S, D), f32)
    for b in range(B):
        out[b] = np.asarray(res.results[b]["outT"]).T
    return out
```

```python
import numpy as np
from contextlib import ExitStack
import concourse.bass as bass
import concourse.mybir as mybir
from concourse.bass_utils import run_bass_kernel_spmd

F32 = mybir.dt.float32
BF16 = mybir.dt.bfloat16
AF = mybir.ActivationFunctionType
ALU = mybir.AluOpType

ENGS = ("pe", "act", "dve", "pool", "sp")

S = 2048
D = 1024
NT = 4
TS = 512
DFF = 2816
NF = 22
CW = 31
NEG = -30000.0

PC_G1, PC_GM, PC_G2, PC_GF = 0, 8, 16, 24
PC_CB, PC_LG, PC_LB, PC_ONC, PC_ONA = 32, 36, 40, 44, 48
PC_CW = 52
PC_FB = 176
NPRM = 192


class Prog:
    def __init__(self, nc, stack):
        self.nc = nc
        self.stack = stack
        self.ops = {e: [] for e in ENGS}
        self.cnt = {}
        self.sems = {}
        self.lastw = {}
        self.readers = {}
        self.base = ()
        for e in ENGS:
            self._sem(e)

    def _sem(self, key):
        if key not in self.sems:
            self.sems[key] = self.stack.enter_context(self.nc.semaphore("s%d" % len(self.sems)))
            self.cnt[key] = 0
        return self.sems[key]

    def barrier(self):
        self.base = tuple((k, v) for k, v in self.cnt.items() if v > 0)

    def _deps(self, reads, writes, extra):
        deps = set(self.base)
        for r in reads:
            t = self.lastw.get(r)
            if t is not None:
                deps.add(t)
        for w in writes:
            t = self.lastw.get(w)
            if t is not None:
                deps.add(t)
            for t in self.readers.get(w, ()):
                deps.add(t)
        for t in extra:
            if t is not None:
                deps.add(t)
        return deps

    def _commit(self, tok, reads, writes):
        for r in reads:
            self.readers.setdefault(r, []).append(tok)
        for w in writes:
            self.lastw[w] = tok
            self.readers[w] = []

    def op(self, eng, fn, reads=(), writes=(), extra=()):
        deps = self._deps(reads, writes, extra)
        self.cnt[eng] += 1
        tok = (eng, self.cnt[eng])
        self.ops[eng].append((deps, fn, tok, 1))
        self._commit(tok, reads, writes)
        return tok

    def dma(self, eng, fn, semkey, reads=(), writes=(), extra=()):
        self._sem(semkey)
        deps = self._deps(reads, writes, extra)
        self.cnt[semkey] += 16
        tok = (semkey, self.cnt[semkey])
        self.ops[eng].append((deps, fn, tok, 16))
        self._commit(tok, reads, writes)
        return tok

    def run(self, final_tokens):
        nc = self.nc
        engmap = {"pe": "tensor", "act": "scalar", "dve": "vector", "pool": "gpsimd", "sp": "sync"}
        with nc.Block() as block:
            for e in ENGS:
                ops = self.ops[e]

                def body(engine, ops=ops, e=e):
                    waited = {}
                    for deps, fn, tok, inc in ops:
                        best = {}
                        for (k, v) in deps:
                            if best.get(k, 0) < v:
                                best[k] = v
                        for k, v in best.items():
                            if k == e and e == "pe":
                                continue
                            if waited.get(k, 0) < v:
                                engine.wait_ge(self.sems[k], v)
                                waited[k] = v
                        ins = fn(engine)
                        ins.then_inc(self.sems[tok[0]], inc)
                    if e == "sp":
                        for (k, v) in final_tokens:
                            engine.wait_ge(self.sems[k], v)

                getattr(block, engmap[e])(body)


def build_program(stop_after=None):
    nc = bass.Bass("TRN2", target_bir_lowering=False)

    def din(name, shape):
        return nc.dram_tensor(name, shape, F32, kind="ExternalInput").ap()

    xT = din("xT", [D, S])
    prm = din("prm", [128, NPRM])
    w13d = [din("w13a", [NF, 128, 2048]), din("w13b", [NF, 128, 2048])]
    w2d = [din("w2a", [NF, 128, 1024]), din("w2b", [NF, 128, 1024])]
    wind = din("win", [16, 128, 1024])
    wvd = din("wv", [2, 128, 2048])
    wfd = din("wf", [128, 64])
    wod = din("wo", [8, 128, 1024])
    outT = nc.dram_tensor("outT", [D, S], F32, kind="ExternalOutput").ap()
    xTv = xT.rearrange("(c p) t -> p c t", p=128)
    outTv = outT.rearrange("(c p) t -> p c t", p=128)

    with ExitStack() as st:
        P = Prog(nc, st)
        PSB = [st.enter_context(nc.psum_tensor("psb%d" % i, [128, 2, 512], F32)) for i in range(4)]

        class _Bank:
            def __init__(self, b):
                self.b = b

            def __getitem__(self, key):
                r, c_ = key
                return PSB[self.b // 2][r, self.b % 2, c_]
        PS = [_Bank(i) for i in range(8)]

        BASE = 16640
        LIMIT = 229376 - 64
        cur = [BASE]

        def T(name, shape, dt, at=None):
            n = 1
            for s_ in shape[1:]:
                n *= s_
            nbytes = n * (4 if dt == F32 else 2)
            nbytes = (nbytes + 63) // 64 * 64
            if at is None:
                off = cur[0]
                cur[0] += nbytes
            else:
                off = at
            assert off + nbytes <= LIMIT, (name, off, nbytes)
            return nc.alloc_sbuf_tensor_at(name, shape, dt, offset=off)

        X = T("X", [128, 8, S], F32)
        PRM = T("PRM", [128, NPRM], F32)
        GS = T("GS", [128, 48], F32)
        EPS = T("EPS", [128, 4], F32)
        ONES = T("ONES", [128, 128], BF16)
        IDENT = T("IDENT", [128, 128], BF16)
        MASK = T("MASK", [128, 128], BF16)
        ZER = T("ZER", [128, 128], BF16)
        MISC_T = T("MISC_T", [128, TS], F32)
        cur[0] = BASE + 65536 + 4096
        M0 = cur[0]
        AVAIL = LIMIT - M0

        def stage_alloc():
            pos = [M0]

            def A(name, shape, dt):
                n = 1
                for s_ in shape[1:]:
                    n *= s_
                nbytes = (n * (4 if dt == F32 else 2) + 63) // 64 * 64
                off = pos[0]
                pos[0] += nbytes
                assert pos[0] <= LIMIT, (name, pos[0] - M0, AVAIL)
                return nc.alloc_sbuf_tensor_at(name, shape, dt, offset=off)
            A.pos = pos
            return A

        uid = [0]

        def nm(s_):
            uid[0] += 1
            return "%s_%d" % (s_, uid[0])

        def tsl(t):
            return slice(t * TS, (t + 1) * TS)

        P.dma("sp", lambda e: e.dma_start(out=PRM[:], in_=prm), "d_prm", writes=["PRM"])
        for t in range(NT):
            P.dma("sp", lambda e, t=t: e.dma_start(out=X[:, :, tsl(t)], in_=xTv[:, :, tsl(t)]), ("d_x", t),
                  writes=[("X", c, t) for c in range(8)])
        P.op("pool", lambda e: e.memset(ONES[:], 1.0), writes=["ONES"])
        P.op("pool", lambda e: e.memset(ZER[:], 0.0), writes=["ZER"])
        P.op("pool", lambda e: e.affine_select(out=IDENT[:], in_=ONES[:], pattern=[[1, 128]], compare_op=ALU.is_equal,
                                               fill=0.0, base=0, channel_multiplier=-1), reads=["ONES"], writes=["IDENT"])
        P.op("pool", lambda e: e.affine_select(out=MASK[:], in_=ZER[:], pattern=[[1, 128]], compare_op=ALU.is_ge,
                                               fill=NEG, base=0, channel_multiplier=-1), reads=["ZER"], writes=["MASK"])
        P.op("pool", lambda e: e.memset(EPS[:, 0:1], 1e-6 * D), writes=["EPS0"])
        P.op("pool", lambda e: e.memset(EPS[:, 1:2], 1e-6 * 512), writes=["EPS1"])
        P.op("pool", lambda e: e.memset(EPS[:, 2:3], 1e-6), writes=["EPS2"])
        P.op("dve", lambda e: e.tensor_scalar(out=GS[:, 0:32], in0=PRM[:, 0:32], scalar1=float(np.sqrt(D)), scalar2=None,
                                              op0=ALU.mult), reads=["PRM"], writes=["GS0"])
        P.op("dve", lambda e: e.tensor_scalar(out=GS[:, 32:40], in0=PRM[:, PC_ONC:PC_ONC + 8], scalar1=float(np.sqrt(512.0)),
                                              scalar2=None, op0=ALU.mult), reads=["PRM"], writes=["GS1"])
        GSR = ["GS0", "GS1", "EPS0", "EPS1", "EPS2"]

        rot = {}

        def rotate(key, lst):
            i = rot.get(key, 0)
            rot[key] = i + 1
            return lst[i % len(lst)]

        def stat_rstd(sq_fn, nch, sq_reads, bank, lnt, rs, eps_col, tag):
            def mm(e):
                ins = None
                for c in range(nch):
                    ins = e.matmul(PS[bank][:, :], ONES[:, :], sq_fn(c), start=(c == 0), stop=(c == nch - 1))
                return ins
            P.op("pe", mm, reads=list(sq_reads) + ["ONES"], writes=[("ps", bank)])
            P.op("act", lambda e: e.activation(out=lnt[:, :], in_=PS[bank][:, :], func=AF.Ln,
                                               bias=EPS[:, eps_col:eps_col + 1], scale=1.0),
                 reads=[("ps", bank)] + GSR, writes=[(tag, "lnt"), (tag, "rs")])
            P.op("act", lambda e: e.activation(out=rs[:, :], in_=lnt[:, :], func=AF.Exp, scale=-0.5),
                 reads=[(tag, "lnt")], writes=[(tag, "rs")])

        def x_norm_ops(t, gcol, SQ, LNT, RS, dst_fn, dst_region, banks):
            b = t % 2
            sq = SQ[b]
            ops = []
            for c in range(8):
                ops.append(lambda c=c: P.op("act", lambda e: e.activation(out=sq[:, c, :], in_=X[:, c, tsl(t)], func=AF.Square),
                                            reads=[("X", c, t)], writes=[("SQ", sq.name, c)]))

            def st_():
                bank = rotate("statb", banks)
                stat_rstd(lambda c: sq[:, c, :], 8, [("SQ", sq.name, c) for c in range(8)], bank, LNT[b], RS[b], 0, ("xn", b))
            ops.append(st_)
            for c in range(8):
                ops.append(lambda c=c: P.op("dve", lambda e: e.scalar_tensor_tensor(
                    out=dst_fn(c), in0=X[:, c, tsl(t)], scalar=GS[:, gcol + c:gcol + c + 1], in1=RS[b][:, :],
                    op0=ALU.mult, op1=ALU.mult),
                    reads=[("X", c, t), (("xn", b), "rs")] + GSR, writes=[dst_region(c)]))
            return ops

        def x_norm(t, gcol, SQ, LNT, RS, dst_fn, dst_region, banks):
            for f_ in x_norm_ops(t, gcol, SQ, LNT, RS, dst_fn, dst_region, banks):
                f_()

        def ffn_stage(which):
            P.barrier()
            A = stage_alloc()
            H = A(nm("H"), [128, 8, S], BF16)
            R13 = [A(nm("R13"), [128, 2048], BF16) for _ in range(8)]
            R2 = [A(nm("R2"), [128, 1024], BF16) for _ in range(9)]
            ACTB = [A(nm("AB"), [128, 6, TS], BF16) for _ in range(2)]
            SG = [A(nm("SG"), [128, TS], F32) for _ in range(2)]
            SQ = [A(nm("SQ"), [128, 8, TS], BF16) for _ in range(2)]
            LNT = [A(nm("LNT"), [128, TS], F32) for _ in range(2)]
            RS = [A(nm("RS"), [128, TS], F32) for _ in range(2)]
            gcol = PC_G1 if which == 0 else PC_G2
            w13 = w13d[which]
            w2 = w2d[which]
            sfx = "f%d" % which
            def RG(*a):
                return (sfx,) + a

            groups = [list(range(0, 6)), list(range(6, 12)), list(range(12, 17)), list(range(17, 22))]
            st13 = {"next": 0, "slot_of": {}, "occ": [None] * 8, "done": set()}
            st2 = {"next": 0, "slot_of": {}, "occ": [None] * 9, "done": set()}

            def pump():
                progressed = True
                while progressed:
                    progressed = False
                    for stt, ring, wsrc, key, nslot in ((st13, R13, w13, "d13", 8), (st2, R2, w2, "d2", 9)):
                        fi = stt["next"]
                        if fi >= NF:
                            continue
                        if stt is st2 and st13["next"] <= fi and st13["next"] < NF:
                            continue
                        s_ = fi % nslot
                        prev = stt["occ"][s_]
                        if prev is not None and prev not in stt["done"]:
                            continue
                        P.dma("pool", lambda e, s_=s_, fi=fi, ring=ring, wsrc=wsrc: e.dma_start(out=ring[s_][:, :], in_=wsrc[fi]),
                              (key, s_), writes=[RG(key, s_)])
                        stt["occ"][s_] = fi
                        stt["slot_of"][fi] = s_
                        stt["next"] = fi + 1
                        progressed = True

            normq = []

            def drip(n):
                for _ in range(n):
                    if not normq:
                        return
                    normq.pop(0)()

            def hnorm(t, gc):
                normq.extend(x_norm_ops(t, gc, SQ, LNT, RS, lambda c, t=t: H[:, c, tsl(t)], lambda c, t=t: RG("H", c, t), [6, 7]))

            def tail_ops(t):
                if which == 0:
                    return x_norm_ops(t, PC_GM, SQ, LNT, RS, lambda c, t=t: H[:, c, tsl(t)], lambda c, t=t: RG("H", c, t), [6, 7])
                ops_ = x_norm_ops(t, PC_GF, SQ, LNT, RS, lambda c, t=t: X[:, c, tsl(t)], lambda c, t=t: ("X", c, t), [6, 7])
                if t < NT - 1:
                    ops_.append(lambda t=t: P.dma("sp", lambda e: e.dma_start(out=outTv[:, :, tsl(t)], in_=X[:, :, tsl(t)]),
                                                  ("d_out", t % 2), reads=[("X", c, t) for c in range(8)]))
                    return ops_
                res = ops_[:9]
                for c_ in range(8):
                    res.append(ops_[9 + c_])
                    res.append(lambda c_=c_: P.dma("sp", lambda e: e.dma_start(out=outTv[:, c_, tsl(t)], in_=X[:, c_, tsl(t)]),
                                                   ("d_out", t % 2), reads=[("X", c_, t)]))
                return res
            hnorm(0, gcol)
            drip(1000)
            pump()

            ucount = [0]

            def U(g, t, i, fi):
                n = ucount[0]
                ucount[0] += 1
                b = n % 2
                s13 = st13["slot_of"][fi]
                pg, pu = 0 + b, 2 + b
                ab = t % 2

                def mm(e):
                    ins = None
                    for kc in range(8):
                        ins = e.matmul(PS[pg][:, :], R13[s13][:, kc * 256:kc * 256 + 128], H[:, kc, tsl(t)],
                                       start=(kc == 0), stop=(kc == 7))
                    for kc in range(8):
                        ins = e.matmul(PS[pu][:, :], R13[s13][:, kc * 256 + 128:kc * 256 + 256], H[:, kc, tsl(t)],
                                       start=(kc == 0), stop=(kc == 7))
                    return ins
                if n == 0:
                    for kc in range(8):
                        P.op("pe", lambda e, kc=kc: e.matmul(PS[pg][:, :], R13[s13][:, kc * 256:kc * 256 + 128], H[:, kc, tsl(t)],
                                                             start=(kc == 0), stop=(kc == 7)),
                             reads=[RG("d13", s13), RG("H", kc, t)], writes=[("ps", pg)])

                    def mm_up(e):
                        ins = None
                        for kc in range(8):
                            ins = e.matmul(PS[pu][:, :], R13[s13][:, kc * 256 + 128:kc * 256 + 256], H[:, kc, tsl(t)],
                                           start=(kc == 0), stop=(kc == 7))
                        return ins
                    P.op("pe", mm_up, reads=[RG("d13", s13)] + [RG("H", c, t) for c in range(8)], writes=[("ps", pu)])
                else:
                    P.op("pe", mm, reads=[RG("d13", s13)] + [RG("H", c, t) for c in range(8)], writes=[("ps", pg), ("ps", pu)])
                P.op("act", lambda e: e.activation(out=SG[b][:, :], in_=PS[pg][:, :], func=AF.Silu),
                     reads=[("ps", pg)], writes=[RG("SG", b)])
                P.op("dve", lambda e: e.tensor_tensor(out=ACTB[ab][:, i, :], in0=SG[b][:, :], in1=PS[pu][:, :], op=ALU.mult),
                     reads=[RG("SG", b), ("ps", pu)], writes=[RG("A", ab, i)])
                if t == NT - 1:
                    st13["done"].add(fi)

            def O(g, t, grp):
                ab = t % 2
                tops = None
                if g == len(groups) - 1:
                    if t >= 2:
                        drip(1000)
                    tops = tail_ops(t)
                for j in range(8):
                    po = rotate("po", [4, 5])

                    def mm(e, j=j, po=po):
                        ins = None
                        for i, fi in enumerate(grp):
                            s2 = st2["slot_of"][fi]
                            ins = e.matmul(PS[po][:, :], R2[s2][:, j * 128:(j + 1) * 128], ACTB[ab][:, i, :],
                                           start=(i == 0), stop=(i == len(grp) - 1))
                        return ins
                    P.op("pe", mm, reads=[RG("d2", st2["slot_of"][fi]) for fi in grp] + [RG("A", ab, i) for i in range(len(grp))],
                         writes=[("ps", po)])
                    P.op("dve", lambda e, j=j, po=po: e.scalar_tensor_tensor(out=X[:, j, tsl(t)], in0=PS[po][:, :], scalar=0.5,
                                                                              in1=X[:, j, tsl(t)], op0=ALU.mult, op1=ALU.add),
                         reads=[("ps", po), ("X", j, t)], writes=[("X", j, t)])
                    if tops is not None:
                        tops[j]()
                if t == NT - 1:
                    for fi in grp:
                        st2["done"].add(fi)
                if tops is not None:
                    normq.extend(tops[8:])

            pending = None
            for g, grp in enumerate(groups):
                for t in range(NT):
                    for i, fi in enumerate(grp):
                        pump()
                        if g == 0 and i == 0:
                            drip(1000)
                        U(g, t, i, fi)
                        drip(4)
                        if g == 0 and i == 0 and t + 1 < NT:
                            hnorm(t + 1, gcol)
                        if pending is not None:
                            O(*pending)
                            pending = None
                        pump()
                    pending = (g, t, grp)
            if pending is not None:
                O(*pending)
            drip(1000)
            assert st13["next"] == NF and st2["next"] == NF
            return H

        def mixer_stage(H):
            P.barrier()
            A = stage_alloc()
            A.pos[0] = M0 + 32768
            YNC = A("mYNC", [128, 4, S], BF16)
            mark = A.pos[0]
            U_ = A("mU", [128, 4, 32 + S], BF16)
            DG = A("mDG", [128, 4 * CW, 128], BF16)
            p_wb = A.pos[0]
            WB = [A("mWB%d" % i, [128, 1024], BF16) for i in range(4)]
            p_sq = A.pos[0]
            SQ = [A("mSQ%d" % i, [128, 8, TS], BF16) for i in range(1)]
            LNT = [A("mLNT%d" % i, [128, TS], F32) for i in range(2)]
            RS = [A("mRS%d" % i, [128, TS], F32) for i in range(2)]
            p_sig = A.pos[0]
            SIG = [A("mSIG%d" % i, [128, TS], F32) for i in range(2)]
            YB = A("mYB", [128, 4, TS], BF16)
            YSQ = A("mYSQ", [128, 4, TS], BF16)
            SW = A("mSW", [128, 4, TS], F32)
            YCT = nc.alloc_sbuf_tensor_at("mYCT", [128, 4, TS], F32, offset=p_sq)
            MU = nc.alloc_sbuf_tensor_at("mMU", [128, TS], F32, offset=p_sig)
            VAR = nc.alloc_sbuf_tensor_at("mVAR", [128, TS], F32, offset=p_sig + 2048)
            PADC = 2

            def RG(*a):
                return ("mx",) + a


            wst = {"n": 0, "occ": [None] * 4, "done": set(), "slot": {}}

            def load_win(ci, WBl, keyp, extra=()):
                s_ = wst["n"] % 4
                wst["n"] += 1
                P.dma("pool", lambda e: e.dma_start(out=WBl[s_][:, :], in_=wind[ci]), (keyp, s_), writes=[RG(keyp, s_)],
                      extra=extra)
                wst["slot"][ci] = s_
                return s_

            P.op("pool", lambda e: e.memset(U_[:, :, 0:32], 0.0), writes=[RG("Upad")])
            dg_list = [(c, j) for c in range(4) for j in range(CW)]

            def dg_emit(n):
                for _ in range(n):
                    if not dg_list:
                        return
                    c, j = dg_list.pop(0)
                    if j % 2 == 0:
                        P.op("act", lambda e, c=c, j=j: e.activation(out=DG[:, c * CW + j, :], in_=IDENT[:, :], func=AF.Copy,
                                                                     scale=PRM[:, PC_CW + c * CW + j:PC_CW + c * CW + j + 1]),
                             reads=["IDENT", "PRM"], writes=[RG("DG", c, j)])
                    else:
                        P.op("dve", lambda e, c=c, j=j: e.tensor_scalar(out=DG[:, c * CW + j, :], in0=IDENT[:, :],
                                                                        scalar1=PRM[:, PC_CW + c * CW + j:PC_CW + c * CW + j + 1],
                                                                        scalar2=None, op0=ALU.mult),
                             reads=["IDENT", "PRM"], writes=[RG("DG", c, j)])

            def proj_chunk(ci, t, bank, WBl, keyp):
                s_ = wst["slot"][ci]

                def mm(e):
                    ins = None
                    for kc in range(8):
                        ins = e.matmul(PS[bank][:, :], WBl[s_][:, kc * 128:(kc + 1) * 128], H[:, kc, tsl(t)],
                                       start=(kc == 0), stop=(kc == 7))
                    return ins
                P.op("pe", mm, reads=[RG(keyp, s_)] + [RG("H", c, t) for c in range(8)], writes=[("ps", bank)])

            for c in range(4):
                load_win(4 + c, WB, "dwb")
                load_win(c, WB, "dwb")
                for t in range(NT):
                    bg = rotate("pa", [0, 1, 2, 3])
                    ba = rotate("pa", [0, 1, 2, 3])
                    proj_chunk(4 + c, t, bg, WB, "dwb")
                    proj_chunk(c, t, ba, WB, "dwb")
                    sb_ = rotate("sig", [0, 1])
                    P.op("act", lambda e, bg=bg, sb_=sb_: e.activation(out=SIG[sb_][:, :], in_=PS[bg][:, :], func=AF.Sigmoid),
                         reads=[("ps", bg)], writes=[RG("SIG", sb_)])
                    P.op("dve", lambda e, ba=ba, sb_=sb_, c=c, t=t: e.tensor_tensor(
                        out=U_[:, c, PADC + 30 + t * TS:PADC + 30 + (t + 1) * TS], in0=SIG[sb_][:, :], in1=PS[ba][:, :], op=ALU.mult),
                        reads=[RG("SIG", sb_), ("ps", ba)], writes=[RG("U", c, t)])
                    dg_emit(8)
            dg_emit(1000)
            ONE64 = nc.alloc_sbuf_tensor_at("mO64", [128, 64], BF16, offset=mark + 93440)
            WF = nc.alloc_sbuf_tensor_at("mWF", [128, 64], BF16, offset=mark + 93568)
            assert mark + 93696 <= LIMIT
            P.op("pool", lambda e: e.memset(ONE64[:], 1.0), writes=[RG("O64")])
            P.dma("pool", lambda e: e.dma_start(out=WF[:, :], in_=wfd), "d_wf", writes=[RG("WF")])

            YCT2 = nc.alloc_sbuf_tensor_at("mYCT2", [128, 4, TS], F32, offset=p_wb)
            YCTS = [YCT, YCT2]

            conv_banks = {}

            def conv_pe(t):
                conv_banks[t] = []
                for c in range(4):
                    bank = rotate("pc", [0, 1, 5, 6])
                    conv_banks[t].append(bank)

                    def mm(e, c=c, t=t, bank=bank):
                        ins = None
                        for j in range(CW):
                            o = PADC + t * TS + j
                            ins = e.matmul(PS[bank][:, :], DG[:, c * CW + j, :], U_[:, c, o:o + TS],
                                           start=(j == 0), stop=(j == CW - 1))
                        return ins
                    rd = [RG("DG", c, j) for j in range(CW)] + [RG("U", c, t), RG("Upad")]
                    if t > 0:
                        rd.append(RG("U", c, t - 1))
                    P.op("pe", mm, reads=rd, writes=[("ps", bank)])

            def conv_evac(t):
                Y = YCTS[t % 2]
                for c in range(4):
                    bank = conv_banks[t][c]
                    P.op("act", lambda e, c=c, bank=bank, Y=Y: e.activation(out=Y[:, c, :], in_=PS[bank][:, :], func=AF.Identity,
                                                                            bias=PRM[:, PC_CB + c:PC_CB + c + 1], scale=1.0),
                         reads=[("ps", bank), "PRM"], writes=[RG("YCT", t % 2, c)])

            def conv_act_yb(t):
                Y = YCTS[t % 2]
                P.op("act", lambda e: e.activation(out=YB[:, :, :], in_=Y[:, :, :], func=AF.Copy),
                     reads=[RG("YCT", t % 2, c) for c in range(4)], writes=[RG("YB")])

            def conv_act_ysq(t):
                Y = YCTS[t % 2]
                P.op("act", lambda e: e.activation(out=YSQ[:, :, :], in_=Y[:, :, :], func=AF.Square),
                     reads=[RG("YCT", t % 2, c) for c in range(4)], writes=[RG("YSQ")])

            def conv_chain_act(t):
                conv_act_yb(t)
                conv_act_ysq(t)

            def conv_chain(t):
                bm, bq = 2, 3

                def mm2(e):
                    ins = None
                    for c in range(4):
                        ins = e.matmul(PS[bm][:, :], ONES[:, :], YB[:, c, :], start=(c == 0), stop=(c == 3))
                    for c in range(4):
                        ins = e.matmul(PS[bq][:, :], ONES[:, :], YSQ[:, c, :], start=(c == 0), stop=(c == 3))
                    return ins
                P.op("pe", mm2, reads=[RG("YB"), RG("YSQ"), "ONES"], writes=[("ps", bm), ("ps", bq)])

            def conv_chain_b_steps(t):
                Y = YCTS[t % 2]
                yb_ = t % 2
                bm, bq = 2, 3
                st_ = []
                if t + 1 < NT:
                    st_.append(lambda: conv_act_yb(t + 1))
                st_.append(lambda: P.op("dve", lambda e: e.tensor_scalar(out=MU[:, :], in0=PS[bm][:, :], scalar1=1.0 / 512,
                                                                         scalar2=None, op0=ALU.mult),
                                        reads=[("ps", bm)], writes=[RG("MU")]))
                st_.append(lambda: P.op("dve", lambda e: e.tensor_tensor(out=VAR[:, :], in0=MU[:, :], in1=MU[:, :], op=ALU.mult),
                                        reads=[RG("MU")], writes=[RG("VAR")]))
                st_.append(lambda: P.op("dve", lambda e: e.scalar_tensor_tensor(out=VAR[:, :], in0=PS[bq][:, :], scalar=1.0 / 512,
                                                                                in1=VAR[:, :], op0=ALU.mult, op1=ALU.subtract),
                                        reads=[("ps", bq), RG("VAR")], writes=[RG("VAR")]))
                st_.append(lambda: P.op("act", lambda e: e.activation(out=LNT[0][:, :], in_=VAR[:, :], func=AF.Ln, bias=EPS[:, 2:3],
                                                                      scale=1.0),
                                        reads=[RG("VAR")] + GSR, writes=[RG("cl"), RG("crs")]))
                st_.append(lambda: P.op("act", lambda e: e.activation(out=RS[0][:, :], in_=LNT[0][:, :], func=AF.Exp, scale=-0.5),
                                        reads=[RG("cl")], writes=[RG("crs")]))
                for c in range(4):
                    st_.append(lambda c=c: P.op("dve", lambda e: e.tensor_tensor(out=Y[:, c, :], in0=Y[:, c, :], in1=MU[:, :],
                                                                                 op=ALU.subtract),
                                                reads=[RG("YCT", yb_, c), RG("MU")], writes=[RG("YCT", yb_, c)]))
                    st_.append(lambda c=c: P.op("dve", lambda e: e.tensor_tensor(out=Y[:, c, :], in0=Y[:, c, :], in1=RS[0][:, :],
                                                                                 op=ALU.mult),
                                                reads=[RG("YCT", yb_, c), RG("crs")], writes=[RG("YCT", yb_, c)]))
                    st_.append(lambda c=c: P.op("act", lambda e: e.activation(out=SW[:, c, :], in_=Y[:, c, :], func=AF.Silu,
                                                                              bias=PRM[:, PC_LB + c:PC_LB + c + 1],
                                                                              scale=PRM[:, PC_LG + c:PC_LG + c + 1]),
                                                reads=[RG("YCT", yb_, c), "PRM"], writes=[RG("SW", c)]))
                st_.append(lambda: P.op("act", lambda e: e.activation(out=YSQ[:, :, :], in_=SW[:, :, :], func=AF.Square),
                                        reads=[RG("SW", c) for c in range(4)], writes=[RG("YSQ")]))
                def c2_pe():
                    def mm(e):
                        ins = None
                        for c in range(4):
                            ins = e.matmul(PS[4][:, :], ONES[:, :], YSQ[:, c, :], start=(c == 0), stop=(c == 3))
                        return ins
                    P.op("pe", mm, reads=[RG("YSQ"), "ONES"], writes=[("ps", 4)])

                def c2_act():
                    tag = RG("c2")
                    P.op("act", lambda e: e.activation(out=LNT[1][:, :], in_=PS[4][:, :], func=AF.Ln, bias=EPS[:, 1:2], scale=1.0),
                         reads=[("ps", 4)] + GSR, writes=[(tag, "lnt"), (tag, "rs")])
                    P.op("act", lambda e: e.activation(out=RS[1][:, :], in_=LNT[1][:, :], func=AF.Exp, scale=-0.5),
                         reads=[(tag, "lnt")], writes=[(tag, "rs")])
                st_.append(c2_pe)
                if t + 1 < NT:
                    st_.append(lambda: conv_act_ysq(t + 1))
                st_.append(c2_act)
                for c in range(4):
                    st_.append(lambda c=c: P.op("dve", lambda e: e.scalar_tensor_tensor(out=YNC[:, c, tsl(t)], in0=SW[:, c, :],
                                                                                        scalar=GS[:, 32 + c:33 + c], in1=RS[1][:, :],
                                                                                        op0=ALU.mult, op1=ALU.mult),
                                                reads=[RG("SW", c), (RG("c2"), "rs")] + GSR, writes=[RG("YNC", c, t)]))
                return st_

            def conv_chain_b(t, other=()):
                other = list(other)
                for i_, f_ in enumerate(conv_chain_b_steps(t)):
                    f_()
                    if other and i_ % 2 == 1:
                        other.pop(0)()
                while other:
                    other.pop(0)()

            QA = nc.alloc_sbuf_tensor_at("mQA", [70, 8, S], BF16, offset=mark)
            WBQ = [nc.alloc_sbuf_tensor_at("mWQ%d" % i, [128, 1024], BF16, offset=mark + 32768 + 2048 * i) for i in range(4)]
            qstate = {}

            def q_proj_steps(c):
                ci = 8 + c
                tk = qstate["tok"]
                P.dma("pool", lambda e: e.dma_start(out=WBQ[c][:, :], in_=wind[ci]), ("dwq", c), writes=[RG("dwq", c)], extra=[tk])
                return [lambda t=t: q_iter(c, t, tk) for t in range(NT)]

            def q_iter(c, t, tk):
                if True:
                    bank = rotate("pq", [7, 0, 1, 5, 6])

                    def mm(e, t=t, bank=bank):
                        ins = None
                        for kc in range(8):
                            ins = e.matmul(PS[bank][:, :], WBQ[c][:, kc * 128:(kc + 1) * 128], H[:, kc, tsl(t)],
                                           start=(kc == 0), stop=(kc == 7))
                        return ins
                    P.op("pe", mm, reads=[RG("dwq", c)] + [RG("H", c_, t) for c_ in range(8)], writes=[("ps", bank)])
                    P.op("act", lambda e, t=t, bank=bank: e.activation(out=QA[0:64, 2 * c, tsl(t)], in_=PS[bank][0:64, :],
                                                                       func=AF.Copy, scale=0.125),
                         reads=[("ps", bank)], writes=[RG("Q", 2 * c, t)], extra=[tk])
                    P.op("dve", lambda e, t=t, bank=bank: e.tensor_scalar(out=QA[0:64, 2 * c + 1, tsl(t)], in0=PS[bank][64:128, :],
                                                                          scalar1=0.125, scalar2=None, op0=ALU.mult),
                         reads=[("ps", bank)], writes=[RG("Q", 2 * c + 1, t)], extra=[tk])

            conv_pe(0)
            conv_evac(0)
            conv_chain_act(0)
            conv_pe(1)
            conv_evac(1)
            for t in range(NT):
                conv_chain(t)
                if t + 2 < NT:
                    conv_pe(t + 2)
                    if t + 2 == NT - 1:
                        qstate["tok"] = ("pe", P.cnt["pe"])
                oth = []
                if t >= 2:
                    oth = q_proj_steps(2 * (t - 2)) + q_proj_steps(2 * (t - 2) + 1)
                conv_chain_b(t, oth)
                if t + 2 < NT:
                    conv_evac(t + 2)

            P.barrier()
            A.pos[0] = mark + 32768
            KA = A("mKA", [70, 8, S], BF16)
            p_vp = A.pos[0]
            VP = A("mVP", [128, 16, 512], BF16)
            p_w = A.pos[0]
            WB2 = [A("mWC%d" % i, [128, 1024], BF16) for i in range(4)]
            p_spare = A.pos[0]
            WVB = [nc.alloc_sbuf_tensor_at("mWV%d" % i, [128, 2048], BF16, offset=p_w + 4096 * i) for i in range(2)]
            SIGF = nc.alloc_sbuf_tensor_at("mSIGF", [8, S], F32, offset=p_vp)
            DD = nc.alloc_sbuf_tensor_at("mDD", [8, S], F32, offset=p_vp + 8192)
            D1 = nc.alloc_sbuf_tensor_at("mD1", [8, S], BF16, offset=p_vp)
            D2 = nc.alloc_sbuf_tensor_at("mD2", [8, S], BF16, offset=p_vp + 4096)


            for t in range(NT):
                bank = rotate("pa", [0, 1, 2, 3])

                def mm(e, t=t, bank=bank):
                    ins = None
                    for kc in range(8):
                        ins = e.matmul(PS[bank][0:8, :], WF[:, kc * 8:(kc + 1) * 8], H[:, kc, tsl(t)], start=(kc == 0), stop=(kc == 7))
                    return ins
                P.op("pe", mm, reads=[RG("WF")] + [RG("H", c, t) for c in range(8)], writes=[("ps", bank)])
                P.op("act", lambda e, t=t, bank=bank: e.activation(out=SIGF[0:8, tsl(t)], in_=PS[bank][0:8, :], func=AF.Sigmoid,
                                                                   bias=PRM[0:8, PC_FB:PC_FB + 1], scale=1.0),
                     reads=[("ps", bank), "PRM"], writes=[RG("SIGF", t)])
            P.op("act", lambda e: e.activation(out=SIGF[0:8, :], in_=SIGF[0:8, :], func=AF.Ln),
                 reads=[RG("SIGF", t) for t in range(NT)], writes=[RG("LOGF")])
            dtoks = []
            pieces = []
            for h_ in range(8):
                for t_ in range(NT):
                    pieces.append(lambda h_=h_, t_=t_: P.op("dve", lambda e: e.memset(QA[64:70, h_, tsl(t_)], -1.0),
                                                            writes=[RG("QAaug")]))
                    pieces.append(lambda h_=h_, t_=t_: P.op("pool", lambda e: e.memset(KA[64:70, h_, tsl(t_)], 1.0),
                                                            writes=[RG("KAaug")]))
            dq = [
                lambda: P.op("dve", lambda e: e.tensor_tensor_scan(out=DD[0:8, :], data0=SIGF[0:8, :], data1=SIGF[0:8, :],
                                                                   initial=0.0, op0=ALU.add, op1=ALU.min),
                             reads=[RG("LOGF")], writes=[RG("DD")]),
                lambda: P.op("dve", lambda e: e.tensor_copy(out=D1[0:8, :], in_=DD[0:8, :]), reads=[RG("DD")], writes=[RG("D1")]),
                lambda: P.op("dve", lambda e: e.tensor_tensor(out=DD[0:8, :], in0=DD[0:8, :], in1=D1[0:8, :], op=ALU.subtract),
                             reads=[RG("DD"), RG("D1")], writes=[RG("DD")]),
                lambda: P.op("dve", lambda e: e.tensor_copy(out=D2[0:8, :], in_=DD[0:8, :]), reads=[RG("DD")], writes=[RG("D2")]),
                lambda: P.op("dve", lambda e: e.tensor_tensor(out=DD[0:8, :], in0=DD[0:8, :], in1=D2[0:8, :], op=ALU.subtract),
                             reads=[RG("DD"), RG("D2")], writes=[RG("DD")]),
            ]

            def d12_dmas():
                while pieces:
                    pieces.pop(0)()
                for i, Di in enumerate((D1, D2)):
                    dtoks.append(P.dma("sp", lambda e, i=i, Di=Di: e.dma_start(out=QA[64 + i:65 + i, :, :], in_=Di[0:8, :]),
                                       ("d_dq", i), reads=[RG("D%d" % (i + 1)), RG("QAaug")], writes=[RG("QAd", i)]))
                    dtoks.append(P.dma("sp", lambda e, i=i, Di=Di: e.dma_start(out=KA[67 + i:68 + i, :, :], in_=Di[0:8, :]),
                                       ("d_dk", i), reads=[RG("D%d" % (i + 1)), RG("KAaug")], writes=[RG("KAd", i)]))
            dq.append(d12_dmas)
            AUGR = [RG("QAd", i) for i in range(3)] + [RG("KAd", i) for i in range(3)]

            def d3_dmas():
                dtoks.append(P.dma("pool", lambda e: e.dma_start(out=QA[66:67, :, :], in_=DD[0:8, :]), ("d_dq", 2),
                                   reads=[RG("DD"), RG("QAaug")], writes=[RG("QAd", 2)]))
                dtoks.append(P.dma("pool", lambda e: e.dma_start(out=KA[69:70, :, :], in_=DD[0:8, :]), ("d_dk", 2),
                                   reads=[RG("DD"), RG("KAaug")], writes=[RG("KAd", 2)]))

            wst["n"] = 0
            wst["slot"] = {}
            ktok = {}
            for qk in range(1, 2):
                for c in range(4):
                    ci = 8 + qk * 4 + c
                    load_win(ci, WB2, "dwc")
                    if c == 3:
                        P.dma("pool", lambda e: e.dma_start(out=WVB[0][:, :], in_=wvd[0]), ("d_wv", 0), writes=[RG("WV", 0)],
                              extra=[ktok[1]])
                    for t in range(NT):
                        bank = rotate("pa", [0, 1, 2, 3])
                        proj_chunk(ci, t, bank, WB2, "dwc")
                        dst = QA if qk == 0 else KA
                        scl = 0.125 if qk == 0 else 1.0
                        nmk = "Q" if qk == 0 else "K"
                        P.op("act", lambda e, dst=dst, c=c, t=t, bank=bank, scl=scl: e.activation(
                            out=dst[0:64, 2 * c, tsl(t)], in_=PS[bank][0:64, :], func=AF.Copy, scale=scl),
                            reads=[("ps", bank)], writes=[RG(nmk, 2 * c, t)])
                        P.op("dve", lambda e, dst=dst, c=c, t=t, bank=bank, scl=scl: e.tensor_scalar(
                            out=dst[0:64, 2 * c + 1, tsl(t)], in0=PS[bank][64:128, :], scalar1=scl, scalar2=None, op0=ALU.mult),
                            reads=[("ps", bank)], writes=[RG(nmk, 2 * c + 1, t)])
                        if t == NT - 1:
                            ktok[c] = ("pe", P.cnt["pe"])
                        for _ in range(4):
                            if pieces:
                                pieces.pop(0)()
                        if dq and (t % 2 == 1):
                            dq.pop(0)()
            while dq:
                dq.pop(0)()
            P.dma("pool", lambda e: e.dma_start(out=WVB[1][:, :], in_=wvd[1]), ("d_wv", 1), writes=[RG("WV", 1)],
                  extra=[ktok[3]])
            d3_dmas()
            for hf in range(2):
                for blk in range(16):
                    bank = rotate("pa", [0, 1, 2, 3])
                    t = blk // 4

                    def mm(e, blk=blk, bank=bank, hf=hf):
                        ins = None
                        for kc in range(8):
                            ins = e.matmul(PS[bank][:, 0:256], H[:, kc, blk * 128:(blk + 1) * 128],
                                           WVB[hf][:, kc * 256:(kc + 1) * 256], start=(kc == 0), stop=(kc == 7))
                        return ins
                    P.op("pe", mm, reads=[RG("WV", hf)] + [RG("H", c, t) for c in range(8)], writes=[("ps", bank)])
                    if blk % 2 == 0:
                        P.op("act", lambda e, blk=blk, bank=bank, hf=hf: e.activation(out=VP[:, blk, hf * 256:(hf + 1) * 256],
                                                                                      in_=PS[bank][:, 0:256], func=AF.Copy),
                             reads=[("ps", bank)], writes=[RG("VPh", blk, hf)], extra=dtoks)
                    else:
                        P.op("dve", lambda e, blk=blk, bank=bank, hf=hf: e.tensor_copy(out=VP[:, blk, hf * 256:(hf + 1) * 256],
                                                                                       in_=PS[bank][:, 0:256]),
                             reads=[("ps", bank)], writes=[RG("VPh", blk, hf)], extra=dtoks)
            tok_lastH = ("pe", P.cnt["pe"])

            A.pos[0] = M0
            WO = [A("mWO%d" % j, [128, 1024], BF16) for j in range(8)]
            YAT = A("mYAT", [128, 4, TS], F32)
            YNA = A("mYNA", [128, 4, TS], BF16)
            SSQ = A("mSSQ", [128, 4, TS], BF16)
            assert A.pos[0] <= M0 + 32768, A.pos[0] - M0
            A.pos[0] = p_w
            PT = [A("mPT%d" % i, [128, 2, TS], BF16) for i in range(3)]
            assert A.pos[0] <= p_w + 8192
            A.pos[0] = p_spare
            RR = A("mRR", [128, TS], F32)
            LN3 = MISC_T
            RS3 = MISC_T
            for j in range(8):
                P.dma("pool", lambda e, j=j: e.dma_start(out=WO[j][:, :], in_=wod[j]), ("d_wo", j), writes=[RG("WO", j)],
                      extra=[tok_lastH])

            pvbanks = {}
            for I in range(NT):
                for c in range(4):
                    pvbanks[(I, c)] = rotate("pvb", [(4, 5), (6, 7)])

            def unit(I, c, j):
                q0 = I * TS
                nkb = 4 * I + 4
                d = j - 4 * I
                n0 = 128 * d if d > 0 else 0
                k = rotate("sbp", [0, 1])
                sbanks = (2 * k, 2 * k + 1)
                pb = rotate("ptb", [0, 1, 2])
                bA, bB = pvbanks[(I, c)]
                last = (j == nkb - 1)

                def qk(e):
                    ins = None
                    for hh in range(2):
                        h = 2 * c + hh
                        sbk = sbanks[hh]
                        e.matmul(PS[sbk][0:64, n0:TS], KA[0:70, h, j * 128:j * 128 + 64], QA[0:70, h, q0 + n0:q0 + TS],
                                 start=True, stop=(d < 0))
                        ins = e.matmul(PS[sbk][64:128, n0:TS], KA[0:70, h, j * 128 + 64:(j + 1) * 128],
                                       QA[0:70, h, q0 + n0:q0 + TS], start=True, stop=(d < 0), tile_position=(0, 64))
                        if d >= 0:
                            e.matmul(PS[sbk][0:64, n0:n0 + 128], IDENT[:, 0:64], MASK[:, :], start=False, stop=True)
                            ins = e.matmul(PS[sbk][64:128, n0:n0 + 128], IDENT[:, 64:128], MASK[:, :], start=False, stop=True,
                                           tile_position=(0, 64))
                    return ins
                rd = ["IDENT", "MASK"] + AUGR
                for hh in range(2):
                    rd += [RG("K", 2 * c + hh, j // 4), RG("Q", 2 * c + hh, I)]
                P.op("pe", qk, reads=rd, writes=[("ps", sbanks[0]), ("ps", sbanks[1])])
                P.op("act", lambda e: e.activation(out=PT[pb][:, :, n0:TS], in_=PSB[k][:, :, n0:TS], func=AF.Exp),
                     reads=[("ps", sbanks[0]), ("ps", sbanks[1])], writes=[RG("PT", pb)])

                def pv():
                    def mm(e):
                        lo, hi = (slice(0, 64), slice(64, 128))
                        vA = VP[:, j, (2 * c) * 64:(2 * c + 1) * 64]
                        vB = VP[:, j, (2 * c + 1) * 64:(2 * c + 2) * 64]
                        e.matmul(PS[bA][lo, n0:TS], vA, PT[pb][:, 0, n0:TS], start=(j == 0), stop=last)
                        e.matmul(PS[bB][hi, n0:TS], ONE64[:, :], PT[pb][:, 0, n0:TS], start=(j == 0), stop=last,
                                 tile_position=(0, 64))
                        e.matmul(PS[bB][lo, n0:TS], ONE64[:, :], PT[pb][:, 1, n0:TS], start=(j == 0), stop=last)
                        return e.matmul(PS[bA][hi, n0:TS], vB, PT[pb][:, 1, n0:TS], start=(j == 0), stop=last,
                                        tile_position=(0, 64))
                    P.op("pe", mm, reads=[RG("PT", pb), RG("VPh", j, 0), RG("VPh", j, 1), RG("O64")], writes=[("ps", bA), ("ps", bB)])
                return pv

            def norm_pair(I, c):
                bA, bB = pvbanks[(I, c)]
                P.op("dve", lambda e: e.reciprocal(out=RR[:, :], in_=PS[bB][:, :]), reads=[("ps", bB)], writes=[RG("RR", 0), RG("RR", 1)])
                P.op("dve", lambda e: e.tensor_tensor(out=YAT[0:64, c, :], in0=PS[bA][0:64, :], in1=RR[64:128, :], op=ALU.mult),
                     reads=[("ps", bA), RG("RR", 1)], writes=[RG("YAT", c, 0)])
                P.op("dve", lambda e: e.tensor_tensor(out=YAT[64:128, c, :], in0=PS[bA][64:128, :], in1=RR[0:64, :], op=ALU.mult),
                     reads=[("ps", bA), RG("RR", 0)], writes=[RG("YAT", c, 1)])

            yr = [RG("YAT", c, k_) for c in range(4) for k_ in range(2)]

            def ssq(I):
                P.op("dve", lambda e: e.tensor_tensor(out=SSQ[:, :, :], in0=YAT[:, :, :], in1=YAT[:, :, :], op=ALU.mult),
                     reads=yr, writes=[RG("SSQ")])

            def epi2(I):
                bst = rotate("sb", [0, 1, 2, 3])
                stat_rstd(lambda c: SSQ[:, c, :], 4, [RG("SSQ")], bst, LN3, RS3, 1, RG("a3"))
                for c in range(4):
                    P.op("dve", lambda e, c=c: e.scalar_tensor_tensor(out=YNA[:, c, :], in0=YAT[:, c, :],
                                                                     scalar=GS[:, 36 + c:37 + c], in1=RS3[:, :],
                                                                     op0=ALU.mult, op1=ALU.mult),
                         reads=[RG("YAT", c, 0), RG("YAT", c, 1), (RG("a3"), "rs")] + GSR, writes=[RG("YNA", c)])
                if stop_after == "mixer_y":
                    P.dma("pool", lambda e: e.dma_start(out=outTv[:, 4:8, tsl(I)], in_=YNA[:, :, :]), ("d_out", 2),
                          reads=[RG("YNA", c) for c in range(4)])
                    P.dma("pool", lambda e: e.dma_start(out=outTv[:, 0:4, tsl(I)], in_=YNC[:, :, tsl(I)]), ("d_out", 3),
                          reads=[RG("YNC", c, I) for c in range(4)])

            def wout(I):
                for jo in range(8):
                    bo = rotate("sb", [0, 1, 2, 3])

                    def mm(e, jo=jo, bo=bo):
                        ins = None
                        for kc in range(8):
                            rhs = YNC[:, kc, tsl(I)] if kc < 4 else YNA[:, kc - 4, :]
                            ins = e.matmul(PS[bo][:, :], WO[jo][:, kc * 128:(kc + 1) * 128], rhs, start=(kc == 0), stop=(kc == 7))
                        return ins
                    P.op("pe", mm, reads=[RG("WO", jo)] + [RG("YNC", c, I) for c in range(4)] + [RG("YNA", c) for c in range(4)],
                         writes=[("ps", bo)])
                    P.op("dve", lambda e, jo=jo, bo=bo: e.tensor_tensor(out=X[:, jo, tsl(I)], in0=PS[bo][:, :], in1=X[:, jo, tsl(I)],
                                                                      op=ALU.add),
                         reads=[("ps", bo), ("X", jo, I)], writes=[("X", jo, I)])

            items = []
            for I in range(NT):
                for c in range(4):
                    for j in range(4 * I + 4):
                        items.append(("unit", I, c, j))
                    hooks = []
                    if c == 0 and I > 0:
                        hooks.append(lambda I=I: epi2(I - 1))
                    hooks.append(lambda I=I, c=c: norm_pair(I, c))
                    if c == 3:
                        hooks.append(lambda I=I: ssq(I))
                    if c == 0 and I > 0:
                        hooks.append(lambda I=I: wout(I - 1))
                    if c == 3 and I == NT - 1:
                        hooks.append(lambda I=I: epi2(I))
                        hooks.append(lambda I=I: wout(I))
                    items.append(("hooks", hooks))
            pendq = []
            for it in items:
                if it[0] == "unit":
                    pvf = unit(it[1], it[2], it[3])
                    pendq.append([pvf, []])
                    if len(pendq) > 2:
                        old = pendq.pop(0)
                        old[0]()
                        for hk in old[1]:
                            hk()
                else:
                    pendq[-1][1].extend(it[1])
            while pendq:
                old = pendq.pop(0)
                old[0]()
                for hk in old[1]:
                    hk()

        def final_stage():
            P.barrier()
            A = stage_alloc()
            SQ = [A("fSQ%d" % i, [128, 8, TS], BF16) for i in range(2)]
            LNT = [A("fLNT%d" % i, [128, TS], F32) for i in range(2)]
            RS = [A("fRS%d" % i, [128, TS], F32) for i in range(2)]
            OT = [A("fOT%d" % i, [128, 8, TS], F32) for i in range(2)]
            for t in range(NT):
                b = t % 2
                x_norm(t, PC_GF, SQ, LNT, RS, lambda c, b=b: OT[b][:, c, :], lambda c, b=b: ("OT", b, c), [6, 7])
                P.dma("sp", lambda e, t=t, b=b: e.dma_start(out=outTv[:, :, tsl(t)], in_=OT[b][:, :, :]), ("d_out", b),
                      reads=[("OT", b, c) for c in range(8)])

        def dump_stage():
            P.barrier()
            for t in range(NT):
                P.dma("sp", lambda e, t=t: e.dma_start(out=outTv[:, :, tsl(t)], in_=X[:, :, tsl(t)]), ("d_out", t % 2),
                      reads=[("X", c, t) for c in range(8)])

        hbox = {}
        stages = [("ffn1", lambda: hbox.__setitem__("H", ffn_stage(0))), ("mixer", lambda: mixer_stage(hbox["H"])),
                  ("ffn2", lambda: ffn_stage(1))]
        stopped = False
        for sname, sfn in stages:
            sfn()
            if stop_after == "mixer_y" and sname == "mixer":
                stopped = True
                break
            if stop_after == sname:
                dump_stage()
                stopped = True
                break
        if stop_after == "ffn1":
            raise RuntimeError("ffn1 dump unsupported with fused norms")
        finals = [(k, v) for k, v in P.cnt.items() if isinstance(k, tuple) and k[0] == "d_out"]
        P.run(finals)
    return nc


_CACHE = {}


def _prep_w13(w13):
    g = w13[:, :DFF].reshape(8, 128, NF, 128)
    u = w13[:, DFF:].reshape(8, 128, NF, 128)
    gu = np.concatenate([g, u], axis=3)
    return np.ascontiguousarray(gu.transpose(2, 1, 0, 3)).reshape(NF, 128, 2048)


def _prep_cols(w, ncol):
    a = w.reshape(8, 128, ncol, 128)
    return np.ascontiguousarray(a.transpose(2, 1, 0, 3)).reshape(ncol, 128, 1024)


def kernel(x, ffn1_norm, ffn1_w13, ffn1_w2, mix_norm, w_in, conv_w, conv_b, conv_ln_g, conv_ln_b, forget_b,
           out_norm_conv, out_norm_attn, w_out, ffn2_norm, ffn2_w13, ffn2_w2, final_norm):
    f32 = np.float32
    x = np.asarray(x, f32)
    B = x.shape[0]

    def vec8(v):
        return np.asarray(v, f32).reshape(8, 128).T

    def vec4(v):
        return np.asarray(v, f32).reshape(4, 128).T

    prm = np.zeros((128, NPRM), f32)
    prm[:, PC_G1:PC_G1 + 8] = vec8(ffn1_norm[0])
    prm[:, PC_GM:PC_GM + 8] = vec8(mix_norm[0])
    prm[:, PC_G2:PC_G2 + 8] = vec8(ffn2_norm[0])
    prm[:, PC_GF:PC_GF + 8] = vec8(final_norm)
    prm[:, PC_CB:PC_CB + 4] = vec4(conv_b[0])
    prm[:, PC_LG:PC_LG + 4] = vec4(conv_ln_g[0])
    prm[:, PC_LB:PC_LB + 4] = vec4(conv_ln_b[0])
    prm[:, PC_ONC:PC_ONC + 4] = vec4(out_norm_conv[0])
    prm[:, PC_ONA:PC_ONA + 4] = vec4(out_norm_attn[0])
    cw = np.asarray(conv_w[0], f32)
    prm[:, PC_CW:PC_CW + 4 * CW] = cw.reshape(CW, 4, 128).transpose(2, 1, 0).reshape(128, 4 * CW)
    prm[0:8, PC_FB] = np.asarray(forget_b[0], f32)

    win = np.asarray(w_in[0], f32)
    shared = {
        "prm": prm,
        "w13a": _prep_w13(np.asarray(ffn1_w13[0], f32)),
        "w2a": np.ascontiguousarray(np.asarray(ffn1_w2[0], f32).reshape(NF, 128, 1024)),
        "w13b": _prep_w13(np.asarray(ffn2_w13[0], f32)),
        "w2b": np.ascontiguousarray(np.asarray(ffn2_w2[0], f32).reshape(NF, 128, 1024)),
        "win": _prep_cols(win[:, 0:2048], 16),
        "wv": np.ascontiguousarray(win[:, 2048:2560].reshape(8, 128, 2, 256).transpose(2, 1, 0, 3)).reshape(2, 128, 2048),
        "wf": np.ascontiguousarray(win[:, 2560:2568].reshape(8, 128, 8).transpose(1, 0, 2)).reshape(128, 64),
        "wo": _prep_cols(np.asarray(w_out[0], f32), 8),
    }
    if "nc" not in _CACHE:
        _CACHE["nc"] = build_program()
    nc = _CACHE["nc"]
    in_maps = []
    for b in range(B):
        m = dict(shared)
        m["xT"] = np.ascontiguousarray(x[b].T)
        in_maps.append(m)
    res = run_bass_kernel_spmd(nc, in_maps, core_ids=list(range(B)))
    out = np.empty((B, S, D), f32)
    for b in range(B):
        out[b] = np.asarray(res.results[b]["outT"]).T
    return out
```

```python
import numpy as np
from contextlib import ExitStack
import concourse.bass as bass
import concourse.mybir as mybir
from concourse.bass_utils import run_bass_kernel_spmd

F32 = mybir.dt.float32
BF16 = mybir.dt.bfloat16
AF = mybir.ActivationFunctionType
ALU = mybir.AluOpType

ENGS = ("pe", "act", "dve", "pool", "sp")

S = 2048
D = 1024
NT = 4
TS = 512
DFF = 2816
NF = 22
CW = 31
NEG = -30000.0

PC_G1, PC_GM, PC_G2, PC_GF = 0, 8, 16, 24
PC_CB, PC_LG, PC_LB, PC_ONC, PC_ONA = 32, 36, 40, 44, 48
PC_CW = 52
PC_FB = 176
NPRM = 192


class Prog:
    def __init__(self, nc, stack):
        self.nc = nc
        self.stack = stack
        self.ops = {e: [] for e in ENGS}
        self.cnt = {}
        self.sems = {}
        self.lastw = {}
        self.readers = {}
        self.base = ()
        for e in ENGS:
            self._sem(e)

    def _sem(self, key):
        if key not in self.sems:
            self.sems[key] = self.stack.enter_context(self.nc.semaphore("s%d" % len(self.sems)))
            self.cnt[key] = 0
        return self.sems[key]

    def barrier(self):
        self.base = tuple((k, v) for k, v in self.cnt.items() if v > 0)

    def _deps(self, reads, writes, extra):
        deps = set(self.base)
        for r in reads:
            t = self.lastw.get(r)
            if t is not None:
                deps.add(t)
        for w in writes:
            t = self.lastw.get(w)
            if t is not None:
                deps.add(t)
            for t in self.readers.get(w, ()):
                deps.add(t)
        for t in extra:
            if t is not None:
                deps.add(t)
        return deps

    def _commit(self, tok, reads, writes):
        for r in reads:
            self.readers.setdefault(r, []).append(tok)
        for w in writes:
            self.lastw[w] = tok
            self.readers[w] = []

    def op(self, eng, fn, reads=(), writes=(), extra=()):
        deps = self._deps(reads, writes, extra)
        self.cnt[eng] += 1
        tok = (eng, self.cnt[eng])
        self.ops[eng].append((deps, fn, tok, 1))
        self._commit(tok, reads, writes)
        return tok

    def dma(self, eng, fn, semkey, reads=(), writes=(), extra=()):
        self._sem(semkey)
        deps = self._deps(reads, writes, extra)
        self.cnt[semkey] += 16
        tok = (semkey, self.cnt[semkey])
        self.ops[eng].append((deps, fn, tok, 16))
        self._commit(tok, reads, writes)
        return tok

    def run(self, final_tokens):
        nc = self.nc
        engmap = {"pe": "tensor", "act": "scalar", "dve": "vector", "pool": "gpsimd", "sp": "sync"}
        with nc.Block() as block:
            for e in ENGS:
                ops = self.ops[e]

                def body(engine, ops=ops, e=e):
                    waited = {}
                    for deps, fn, tok, inc in ops:
                        best = {}
                        for (k, v) in deps:
                            if best.get(k, 0) < v:
                                best[k] = v
                        for k, v in best.items():
                            if k == e and e == "pe":
                                continue
                            if waited.get(k, 0) < v:
                                engine.wait_ge(self.sems[k], v)
                                waited[k] = v
                        ins = fn(engine)
                        ins.then_inc(self.sems[tok[0]], inc)
                    if e == "sp":
                        for (k, v) in final_tokens:
                            engine.wait_ge(self.sems[k], v)

                getattr(block, engmap[e])(body)


def build_program(stop_after=None):
    nc = bass.Bass("TRN2", target_bir_lowering=False)

    def din(name, shape):
        return nc.dram_tensor(name, shape, F32, kind="ExternalInput").ap()

    xT = din("xT", [D, S])
    prm = din("prm", [128, NPRM])
    w13d = [din("w13a", [NF, 128, 2048]), din("w13b", [NF, 128, 2048])]
    w2d = [din("w2a", [NF, 128, 1024]), din("w2b", [NF, 128, 1024])]
    wind = din("win", [16, 128, 1024])
    wvd = din("wv", [2, 128, 2048])
    wfd = din("wf", [128, 64])
    wod = din("wo", [8, 128, 1024])
    outT = nc.dram_tensor("outT", [D, S], F32, kind="ExternalOutput").ap()
    xTv = xT.rearrange("(c p) t -> p c t", p=128)
    outTv = outT.rearrange("(c p) t -> p c t", p=128)

    with ExitStack() as st:
        P = Prog(nc, st)
        PSB = [st.enter_context(nc.psum_tensor("psb%d" % i, [128, 2, 512], F32)) for i in range(4)]

        class _Bank:
            def __init__(self, b):
                self.b = b

            def __getitem__(self, key):
                r, c_ = key
                return PSB[self.b // 2][r, self.b % 2, c_]
        PS = [_Bank(i) for i in range(8)]

        BASE = 16640
        LIMIT = 229376 - 64
        cur = [BASE]

        def T(name, shape, dt, at=None):
            n = 1
            for s_ in shape[1:]:
                n *= s_
            nbytes = n * (4 if dt == F32 else 2)
            nbytes = (nbytes + 63) // 64 * 64
            if at is None:
                off = cur[0]
                cur[0] += nbytes
            else:
                off = at
            assert off + nbytes <= LIMIT, (name, off, nbytes)
            return nc.alloc_sbuf_tensor_at(name, shape, dt, offset=off)

        X = T("X", [128, 8, S], F32)
        PRM = T("PRM", [128, NPRM], F32)
        GS = T("GS", [128, 48], F32)
        EPS = T("EPS", [128, 4], F32)
        ONES = T("ONES", [128, 128], BF16)
        IDENT = T("IDENT", [128, 128], BF16)
        MASK = T("MASK", [128, 128], BF16)
        ZER = T("ZER", [128, 128], BF16)
        MISC_T = T("MISC_T", [128, TS], F32)
        cur[0] = BASE + 65536 + 4096
        M0 = cur[0]
        AVAIL = LIMIT - M0

        def stage_alloc():
            pos = [M0]

            def A(name, shape, dt):
                n = 1
                for s_ in shape[1:]:
                    n *= s_
                nbytes = (n * (4 if dt == F32 else 2) + 63) // 64 * 64
                off = pos[0]
                pos[0] += nbytes
                assert pos[0] <= LIMIT, (name, pos[0] - M0, AVAIL)
                return nc.alloc_sbuf_tensor_at(name, shape, dt, offset=off)
            A.pos = pos
            return A

        uid = [0]

        def nm(s_):
            uid[0] += 1
            return "%s_%d" % (s_, uid[0])

        def tsl(t):
            return slice(t * TS, (t + 1) * TS)

        P.dma("sp", lambda e: e.dma_start(out=PRM[:], in_=prm), "d_prm", writes=["PRM"])
        for t in range(NT):
            P.dma("sp", lambda e, t=t: e.dma_start(out=X[:, :, tsl(t)], in_=xTv[:, :, tsl(t)]), ("d_x", t),
                  writes=[("X", c, t) for c in range(8)])
        P.op("pool", lambda e: e.memset(ONES[:], 1.0), writes=["ONES"])
        P.op("pool", lambda e: e.memset(ZER[:], 0.0), writes=["ZER"])
        P.op("pool", lambda e: e.affine_select(out=IDENT[:], in_=ONES[:], pattern=[[1, 128]], compare_op=ALU.is_equal,
                                               fill=0.0, base=0, channel_multiplier=-1), reads=["ONES"], writes=["IDENT"])
        P.op("pool", lambda e: e.affine_select(out=MASK[:], in_=ZER[:], pattern=[[1, 128]], compare_op=ALU.is_ge,
                                               fill=NEG, base=0, channel_multiplier=-1), reads=["ZER"], writes=["MASK"])
        P.op("pool", lambda e: e.memset(EPS[:, 0:1], 1e-6 * D), writes=["EPS0"])
        P.op("pool", lambda e: e.memset(EPS[:, 1:2], 1e-6 * 512), writes=["EPS1"])
        P.op("pool", lambda e: e.memset(EPS[:, 2:3], 1e-6), writes=["EPS2"])
        P.op("dve", lambda e: e.tensor_scalar(out=GS[:, 0:32], in0=PRM[:, 0:32], scalar1=float(np.sqrt(D)), scalar2=None,
                                              op0=ALU.mult), reads=["PRM"], writes=["GS0"])
        P.op("dve", lambda e: e.tensor_scalar(out=GS[:, 32:40], in0=PRM[:, PC_ONC:PC_ONC + 8], scalar1=float(np.sqrt(512.0)),
                                              scalar2=None, op0=ALU.mult), reads=["PRM"], writes=["GS1"])
        GSR = ["GS0", "GS1", "EPS0", "EPS1", "EPS2"]

        rot = {}

        def rotate(key, lst):
            i = rot.get(key, 0)
            rot[key] = i + 1
            return lst[i % len(lst)]

        def stat_rstd(sq_fn, nch, sq_reads, bank, lnt, rs, eps_col, tag):
            def mm(e):
                ins = None
                for c in range(nch):
                    ins = e.matmul(PS[bank][:, :], ONES[:, :], sq_fn(c), start=(c == 0), stop=(c == nch - 1))
                return ins
            P.op("pe", mm, reads=list(sq_reads) + ["ONES"], writes=[("ps", bank)])
            P.op("act", lambda e: e.activation(out=lnt[:, :], in_=PS[bank][:, :], func=AF.Ln,
                                               bias=EPS[:, eps_col:eps_col + 1], scale=1.0),
                 reads=[("ps", bank)] + GSR, writes=[(tag, "lnt"), (tag, "rs")])
            P.op("act", lambda e: e.activation(out=rs[:, :], in_=lnt[:, :], func=AF.Exp, scale=-0.5),
                 reads=[(tag, "lnt")], writes=[(tag, "rs")])

        def x_norm_ops(t, gcol, SQ, LNT, RS, dst_fn, dst_region, banks):
            b = t % 2
            sq = SQ[b]
            ops = []
            for c in range(8):
                ops.append(lambda c=c: P.op("act", lambda e: e.activation(out=sq[:, c, :], in_=X[:, c, tsl(t)], func=AF.Square),
                                            reads=[("X", c, t)], writes=[("SQ", sq.name, c)]))

            def st_():
                bank = rotate("statb", banks)
                stat_rstd(lambda c: sq[:, c, :], 8, [("SQ", sq.name, c) for c in range(8)], bank, LNT[b], RS[b], 0, ("xn", b))
            ops.append(st_)
            for c in range(8):
                ops.append(lambda c=c: P.op("dve", lambda e: e.scalar_tensor_tensor(
                    out=dst_fn(c), in0=X[:, c, tsl(t)], scalar=GS[:, gcol + c:gcol + c + 1], in1=RS[b][:, :],
                    op0=ALU.mult, op1=ALU.mult),
                    reads=[("X", c, t), (("xn", b), "rs")] + GSR, writes=[dst_region(c)]))
            return ops

        def x_norm(t, gcol, SQ, LNT, RS, dst_fn, dst_region, banks):
            for f_ in x_norm_ops(t, gcol, SQ, LNT, RS, dst_fn, dst_region, banks):
                f_()

        def ffn_stage(which):
            P.barrier()
            A = stage_alloc()
            H = A(nm("H"), [128, 8, S], BF16)
            R13 = [A(nm("R13"), [128, 2048], BF16) for _ in range(8)]
            R2 = [A(nm("R2"), [128, 1024], BF16) for _ in range(9)]
            ACTB = [A(nm("AB"), [128, 6, TS], BF16) for _ in range(2)]
            SG = [A(nm("SG"), [128, TS], F32) for _ in range(2)]
            SQ = [A(nm("SQ"), [128, 8, TS], BF16) for _ in range(2)]
            LNT = [A(nm("LNT"), [128, TS], F32) for _ in range(2)]
            RS = [A(nm("RS"), [128, TS], F32) for _ in range(2)]
            gcol = PC_G1 if which == 0 else PC_G2
            w13 = w13d[which]
            w2 = w2d[which]
            sfx = "f%d" % which
            def RG(*a):
                return (sfx,) + a

            groups = [list(range(0, 6)), list(range(6, 12)), list(range(12, 17)), list(range(17, 22))]
            st13 = {"next": 0, "slot_of": {}, "occ": [None] * 8, "done": set()}
            st2 = {"next": 0, "slot_of": {}, "occ": [None] * 9, "done": set()}

            def pump():
                progressed = True
                while progressed:
                    progressed = False
                    for stt, ring, wsrc, key, nslot in ((st13, R13, w13, "d13", 8), (st2, R2, w2, "d2", 9)):
                        fi = stt["next"]
                        if fi >= NF:
                            continue
                        if stt is st2 and st13["next"] <= fi and st13["next"] < NF:
                            continue
                        s_ = fi % nslot
                        prev = stt["occ"][s_]
                        if prev is not None and prev not in stt["done"]:
                            continue
                        P.dma("pool", lambda e, s_=s_, fi=fi, ring=ring, wsrc=wsrc: e.dma_start(out=ring[s_][:, :], in_=wsrc[fi]),
                              (key, s_), writes=[RG(key, s_)])
                        stt["occ"][s_] = fi
                        stt["slot_of"][fi] = s_
                        stt["next"] = fi + 1
                        progressed = True

            normq = []

            def drip(n):
                for _ in range(n):
                    if not normq:
                        return
                    normq.pop(0)()

            def hnorm(t, gc):
                normq.extend(x_norm_ops(t, gc, SQ, LNT, RS, lambda c, t=t: H[:, c, tsl(t)], lambda c, t=t: RG("H", c, t), [6, 7]))

            def tail_ops(t):
                if which == 0:
                    return x_norm_ops(t, PC_GM, SQ, LNT, RS, lambda c, t=t: H[:, c, tsl(t)], lambda c, t=t: RG("H", c, t), [6, 7])
                ops_ = x_norm_ops(t, PC_GF, SQ, LNT, RS, lambda c, t=t: X[:, c, tsl(t)], lambda c, t=t: ("X", c, t), [6, 7])
                if t < NT - 1:
                    ops_.append(lambda t=t: P.dma("sp", lambda e: e.dma_start(out=outTv[:, :, tsl(t)], in_=X[:, :, tsl(t)]),
                                                  ("d_out", t % 2), reads=[("X", c, t) for c in range(8)]))
                    return ops_
                res = ops_[:9]
                for c_ in range(8):
                    res.append(ops_[9 + c_])
                    res.append(lambda c_=c_: P.dma("sp", lambda e: e.dma_start(out=outTv[:, c_, tsl(t)], in_=X[:, c_, tsl(t)]),
                                                   ("d_out", t % 2), reads=[("X", c_, t)]))
                return res
            hnorm(0, gcol)
            drip(1000)
            pump()

            ucount = [0]

            def U(g, t, i, fi):
                n = ucount[0]
                ucount[0] += 1
                b = n % 2
                s13 = st13["slot_of"][fi]
                pg, pu = 0 + b, 2 + b
                ab = t % 2

                def mm(e):
                    ins = None
                    for kc in range(8):
                        ins = e.matmul(PS[pg][:, :], R13[s13][:, kc * 256:kc * 256 + 128], H[:, kc, tsl(t)],
                                       start=(kc == 0), stop=(kc == 7))
                    for kc in range(8):
                        ins = e.matmul(PS[pu][:, :], R13[s13][:, kc * 256 + 128:kc * 256 + 256], H[:, kc, tsl(t)],
                                       start=(kc == 0), stop=(kc == 7))
                    return ins
                if g == 0 and i == 0:
                    for kc in range(8):
                        P.op("pe", lambda e, kc=kc: e.matmul(PS[pg][:, :], R13[s13][:, kc * 256:kc * 256 + 128], H[:, kc, tsl(t)],
                                                             start=(kc == 0), stop=(kc == 7)),
                             reads=[RG("d13", s13), RG("H", kc, t)], writes=[("ps", pg)])

                    def mm_up(e):
                        ins = None
                        for kc in range(8):
                            ins = e.matmul(PS[pu][:, :], R13[s13][:, kc * 256 + 128:kc * 256 + 256], H[:, kc, tsl(t)],
                                           start=(kc == 0), stop=(kc == 7))
                        return ins
                    P.op("pe", mm_up, reads=[RG("d13", s13)] + [RG("H", c, t) for c in range(8)], writes=[("ps", pu)])
                else:
                    P.op("pe", mm, reads=[RG("d13", s13)] + [RG("H", c, t) for c in range(8)], writes=[("ps", pg), ("ps", pu)])
                P.op("act", lambda e: e.activation(out=SG[b][:, :], in_=PS[pg][:, :], func=AF.Silu),
                     reads=[("ps", pg)], writes=[RG("SG", b)])
                P.op("dve", lambda e: e.tensor_tensor(out=ACTB[ab][:, i, :], in0=SG[b][:, :], in1=PS[pu][:, :], op=ALU.mult),
                     reads=[RG("SG", b), ("ps", pu)], writes=[RG("A", ab, i)])
                if t == NT - 1:
                    st13["done"].add(fi)

            def O(g, t, grp):
                ab = t % 2
                tops = None
                if g == len(groups) - 1:
                    if t >= 2:
                        drip(1000)
                    tops = tail_ops(t)
                for j in range(8):
                    po = rotate("po", [4, 5])

                    def mm(e, j=j, po=po):
                        ins = None
                        for i, fi in enumerate(grp):
                            s2 = st2["slot_of"][fi]
                            ins = e.matmul(PS[po][:, :], R2[s2][:, j * 128:(j + 1) * 128], ACTB[ab][:, i, :],
                                           start=(i == 0), stop=(i == len(grp) - 1))
                        return ins
                    P.op("pe", mm, reads=[RG("d2", st2["slot_of"][fi]) for fi in grp] + [RG("A", ab, i) for i in range(len(grp))],
                         writes=[("ps", po)])
                    P.op("dve", lambda e, j=j, po=po: e.scalar_tensor_tensor(out=X[:, j, tsl(t)], in0=PS[po][:, :], scalar=0.5,
                                                                              in1=X[:, j, tsl(t)], op0=ALU.mult, op1=ALU.add),
                         reads=[("ps", po), ("X", j, t)], writes=[("X", j, t)])
                    if tops is not None:
                        tops[j]()
                if t == NT - 1:
                    for fi in grp:
                        st2["done"].add(fi)
                if tops is not None:
                    normq.extend(tops[8:])

            pending = None
            for g, grp in enumerate(groups):
                for t in range(NT):
                    for i, fi in enumerate(grp):
                        pump()
                        if g == 0 and i == 0:
                            drip(1000)
                        U(g, t, i, fi)
                        drip(4)
                        if g == 0 and i == 0 and t + 1 < NT:
                            hnorm(t + 1, gcol)
                        if pending is not None:
                            O(*pending)
                            pending = None
                        pump()
                    pending = (g, t, grp)
            if pending is not None:
                O(*pending)
            drip(1000)
            assert st13["next"] == NF and st2["next"] == NF
            return H

        def mixer_stage(H):
            P.barrier()
            A = stage_alloc()
            A.pos[0] = M0 + 32768
            YNC = A("mYNC", [128, 4, S], BF16)
            mark = A.pos[0]
            U_ = A("mU", [128, 4, 32 + S], BF16)
            DG = A("mDG", [128, 4 * CW, 128], BF16)
            p_wb = A.pos[0]
            WB = [A("mWB%d" % i, [128, 1024], BF16) for i in range(4)]
            p_sq = A.pos[0]
            SQ = [A("mSQ%d" % i, [128, 8, TS], BF16) for i in range(1)]
            LNT = [A("mLNT%d" % i, [128, TS], F32) for i in range(2)]
            RS = [A("mRS%d" % i, [128, TS], F32) for i in range(2)]
            p_sig = A.pos[0]
            SIG = [A("mSIG%d" % i, [128, TS], F32) for i in range(2)]
            YB = A("mYB", [128, 4, TS], BF16)
            YSQ = A("mYSQ", [128, 4, TS], BF16)
            SW = A("mSW", [128, 4, TS], F32)
            YCT = nc.alloc_sbuf_tensor_at("mYCT", [128, 4, TS], F32, offset=p_sq)
            MU = nc.alloc_sbuf_tensor_at("mMU", [128, TS], F32, offset=p_sig)
            VAR = nc.alloc_sbuf_tensor_at("mVAR", [128, TS], F32, offset=p_sig + 2048)
            PADC = 2

            def RG(*a):
                return ("mx",) + a


            wst = {"n": 0, "occ": [None] * 4, "done": set(), "slot": {}}

            def load_win(ci, WBl, keyp, extra=()):
                s_ = wst["n"] % 4
                wst["n"] += 1
                P.dma("pool", lambda e: e.dma_start(out=WBl[s_][:, :], in_=wind[ci]), (keyp, s_), writes=[RG(keyp, s_)],
                      extra=extra)
                wst["slot"][ci] = s_
                return s_

            P.op("pool", lambda e: e.memset(U_[:, :, 0:32], 0.0), writes=[RG("Upad")])
            dg_list = [(c, j) for c in range(4) for j in range(CW)]

            def dg_emit(n):
                for _ in range(n):
                    if not dg_list:
                        return
                    c, j = dg_list.pop(0)
                    if j % 2 == 0:
                        P.op("act", lambda e, c=c, j=j: e.activation(out=DG[:, c * CW + j, :], in_=IDENT[:, :], func=AF.Copy,
                                                                     scale=PRM[:, PC_CW + c * CW + j:PC_CW + c * CW + j + 1]),
                             reads=["IDENT", "PRM"], writes=[RG("DG", c, j)])
                    else:
                        P.op("dve", lambda e, c=c, j=j: e.tensor_scalar(out=DG[:, c * CW + j, :], in0=IDENT[:, :],
                                                                        scalar1=PRM[:, PC_CW + c * CW + j:PC_CW + c * CW + j + 1],
                                                                        scalar2=None, op0=ALU.mult),
                             reads=["IDENT", "PRM"], writes=[RG("DG", c, j)])

            def proj_chunk(ci, t, bank, WBl, keyp):
                s_ = wst["slot"][ci]

                def mm(e):
                    ins = None
                    for kc in range(8):
                        ins = e.matmul(PS[bank][:, :], WBl[s_][:, kc * 128:(kc + 1) * 128], H[:, kc, tsl(t)],
                                       start=(kc == 0), stop=(kc == 7))
                    return ins
                P.op("pe", mm, reads=[RG(keyp, s_)] + [RG("H", c, t) for c in range(8)], writes=[("ps", bank)])

            for c in range(4):
                load_win(4 + c, WB, "dwb")
                load_win(c, WB, "dwb")
                for t in range(NT):
                    bg = rotate("pa", [0, 1, 2, 3])
                    ba = rotate("pa", [0, 1, 2, 3])
                    proj_chunk(4 + c, t, bg, WB, "dwb")
                    proj_chunk(c, t, ba, WB, "dwb")
                    sb_ = rotate("sig", [0, 1])
                    P.op("act", lambda e, bg=bg, sb_=sb_: e.activation(out=SIG[sb_][:, :], in_=PS[bg][:, :], func=AF.Sigmoid),
                         reads=[("ps", bg)], writes=[RG("SIG", sb_)])
                    P.op("dve", lambda e, ba=ba, sb_=sb_, c=c, t=t: e.tensor_tensor(
                        out=U_[:, c, PADC + 30 + t * TS:PADC + 30 + (t + 1) * TS], in0=SIG[sb_][:, :], in1=PS[ba][:, :], op=ALU.mult),
                        reads=[RG("SIG", sb_), ("ps", ba)], writes=[RG("U", c, t)])
                    dg_emit(8)
            dg_emit(1000)
            ONE64 = nc.alloc_sbuf_tensor_at("mO64", [128, 64], BF16, offset=mark + 93440)
            WF = nc.alloc_sbuf_tensor_at("mWF", [128, 64], BF16, offset=mark + 93568)
            assert mark + 93696 <= LIMIT
            P.op("pool", lambda e: e.memset(ONE64[:], 1.0), writes=[RG("O64")])
            P.dma("pool", lambda e: e.dma_start(out=WF[:, :], in_=wfd), "d_wf", writes=[RG("WF")])

            YCT2 = nc.alloc_sbuf_tensor_at("mYCT2", [128, 4, TS], F32, offset=p_wb)
            YCTS = [YCT, YCT2]

            conv_banks = {}

            def conv_pe(t):
                conv_banks[t] = []
                for c in range(4):
                    bank = rotate("pc", [0, 1, 5, 6])
                    conv_banks[t].append(bank)

                    def mm(e, c=c, t=t, bank=bank):
                        ins = None
                        for j in range(CW):
                            o = PADC + t * TS + j
                            ins = e.matmul(PS[bank][:, :], DG[:, c * CW + j, :], U_[:, c, o:o + TS],
                                           start=(j == 0), stop=(j == CW - 1))
                        return ins
                    rd = [RG("DG", c, j) for j in range(CW)] + [RG("U", c, t), RG("Upad")]
                    if t > 0:
                        rd.append(RG("U", c, t - 1))
                    P.op("pe", mm, reads=rd, writes=[("ps", bank)])

            def conv_evac(t):
                Y = YCTS[t % 2]
                for c in range(4):
                    bank = conv_banks[t][c]
                    P.op("act", lambda e, c=c, bank=bank, Y=Y: e.activation(out=Y[:, c, :], in_=PS[bank][:, :], func=AF.Identity,
                                                                            bias=PRM[:, PC_CB + c:PC_CB + c + 1], scale=1.0),
                         reads=[("ps", bank), "PRM"], writes=[RG("YCT", t % 2, c)])

            def conv_act_yb(t):
                Y = YCTS[t % 2]
                P.op("act", lambda e: e.activation(out=YB[:, :, :], in_=Y[:, :, :], func=AF.Copy),
                     reads=[RG("YCT", t % 2, c) for c in range(4)], writes=[RG("YB")])

            def conv_act_ysq(t):
                Y = YCTS[t % 2]
                P.op("act", lambda e: e.activation(out=YSQ[:, :, :], in_=Y[:, :, :], func=AF.Square),
                     reads=[RG("YCT", t % 2, c) for c in range(4)], writes=[RG("YSQ")])

            def conv_chain_act(t):
                conv_act_yb(t)
                conv_act_ysq(t)

            def conv_chain(t):
                bm, bq = 2, 3

                def mm2(e):
                    ins = None
                    for c in range(4):
                        ins = e.matmul(PS[bm][:, :], ONES[:, :], YB[:, c, :], start=(c == 0), stop=(c == 3))
                    for c in range(4):
                        ins = e.matmul(PS[bq][:, :], ONES[:, :], YSQ[:, c, :], start=(c == 0), stop=(c == 3))
                    return ins
                P.op("pe", mm2, reads=[RG("YB"), RG("YSQ"), "ONES"], writes=[("ps", bm), ("ps", bq)])

            def conv_chain_b_steps(t):
                Y = YCTS[t % 2]
                yb_ = t % 2
                bm, bq = 2, 3
                st_ = []
                if t + 1 < NT:
                    st_.append(lambda: conv_act_yb(t + 1))
                st_.append(lambda: P.op("dve", lambda e: e.tensor_scalar(out=MU[:, :], in0=PS[bm][:, :], scalar1=1.0 / 512,
                                                                         scalar2=None, op0=ALU.mult),
                                        reads=[("ps", bm)], writes=[RG("MU")]))
                st_.append(lambda: P.op("dve", lambda e: e.tensor_tensor(out=VAR[:, :], in0=MU[:, :], in1=MU[:, :], op=ALU.mult),
                                        reads=[RG("MU")], writes=[RG("VAR")]))
                st_.append(lambda: P.op("dve", lambda e: e.scalar_tensor_tensor(out=VAR[:, :], in0=PS[bq][:, :], scalar=1.0 / 512,
                                                                                in1=VAR[:, :], op0=ALU.mult, op1=ALU.subtract),
                                        reads=[("ps", bq), RG("VAR")], writes=[RG("VAR")]))
                st_.append(lambda: P.op("act", lambda e: e.activation(out=LNT[0][:, :], in_=VAR[:, :], func=AF.Ln, bias=EPS[:, 2:3],
                                                                      scale=1.0),
                                        reads=[RG("VAR")] + GSR, writes=[RG("cl"), RG("crs")]))
                st_.append(lambda: P.op("act", lambda e: e.activation(out=RS[0][:, :], in_=LNT[0][:, :], func=AF.Exp, scale=-0.5),
                                        reads=[RG("cl")], writes=[RG("crs")]))
                for c in range(4):
                    st_.append(lambda c=c: P.op("dve", lambda e: e.tensor_tensor(out=Y[:, c, :], in0=Y[:, c, :], in1=MU[:, :],
                                                                                 op=ALU.subtract),
                                                reads=[RG("YCT", yb_, c), RG("MU")], writes=[RG("YCT", yb_, c)]))
                    st_.append(lambda c=c: P.op("dve", lambda e: e.tensor_tensor(out=Y[:, c, :], in0=Y[:, c, :], in1=RS[0][:, :],
                                                                                 op=ALU.mult),
                                                reads=[RG("YCT", yb_, c), RG("crs")], writes=[RG("YCT", yb_, c)]))
                    st_.append(lambda c=c: P.op("act", lambda e: e.activation(out=SW[:, c, :], in_=Y[:, c, :], func=AF.Silu,
                                                                              bias=PRM[:, PC_LB + c:PC_LB + c + 1],
                                                                              scale=PRM[:, PC_LG + c:PC_LG + c + 1]),
                                                reads=[RG("YCT", yb_, c), "PRM"], writes=[RG("SW", c)]))
                st_.append(lambda: P.op("act", lambda e: e.activation(out=YSQ[:, :, :], in_=SW[:, :, :], func=AF.Square),
                                        reads=[RG("SW", c) for c in range(4)], writes=[RG("YSQ")]))
                def c2_pe():
                    def mm(e):
                        ins = None
                        for c in range(4):
                            ins = e.matmul(PS[4][:, :], ONES[:, :], YSQ[:, c, :], start=(c == 0), stop=(c == 3))
                        return ins
                    P.op("pe", mm, reads=[RG("YSQ"), "ONES"], writes=[("ps", 4)])

                def c2_act():
                    tag = RG("c2")
                    P.op("act", lambda e: e.activation(out=LNT[1][:, :], in_=PS[4][:, :], func=AF.Ln, bias=EPS[:, 1:2], scale=1.0),
                         reads=[("ps", 4)] + GSR, writes=[(tag, "lnt"), (tag, "rs")])
                    P.op("act", lambda e: e.activation(out=RS[1][:, :], in_=LNT[1][:, :], func=AF.Exp, scale=-0.5),
                         reads=[(tag, "lnt")], writes=[(tag, "rs")])
                st_.append(c2_pe)
                if t + 1 < NT:
                    st_.append(lambda: conv_act_ysq(t + 1))
                st_.append(c2_act)
                for c in range(4):
                    st_.append(lambda c=c: P.op("dve", lambda e: e.scalar_tensor_tensor(out=YNC[:, c, tsl(t)], in0=SW[:, c, :],
                                                                                        scalar=GS[:, 32 + c:33 + c], in1=RS[1][:, :],
                                                                                        op0=ALU.mult, op1=ALU.mult),
                                                reads=[RG("SW", c), (RG("c2"), "rs")] + GSR, writes=[RG("YNC", c, t)]))
                return st_

            def conv_chain_b(t, other=()):
                other = list(other)
                for i_, f_ in enumerate(conv_chain_b_steps(t)):
                    f_()
                    if other and i_ % 2 == 1:
                        other.pop(0)()
                while other:
                    other.pop(0)()

            QA = nc.alloc_sbuf_tensor_at("mQA", [70, 8, S], BF16, offset=mark)
            WBQ = [nc.alloc_sbuf_tensor_at("mWQ%d" % i, [128, 1024], BF16, offset=mark + 32768 + 2048 * i) for i in range(4)]
            qstate = {}

            def q_proj_steps(c):
                ci = 8 + c
                tk = qstate["tok"]
                P.dma("pool", lambda e: e.dma_start(out=WBQ[c][:, :], in_=wind[ci]), ("dwq", c), writes=[RG("dwq", c)], extra=[tk])
                return [lambda t=t: q_iter(c, t, tk) for t in range(NT)]

            def q_iter(c, t, tk):
                if True:
                    bank = rotate("pq", [7, 0, 1, 5, 6])

                    def mm(e, t=t, bank=bank):
                        ins = None
                        for kc in range(8):
                            ins = e.matmul(PS[bank][:, :], WBQ[c][:, kc * 128:(kc + 1) * 128], H[:, kc, tsl(t)],
                                           start=(kc == 0), stop=(kc == 7))
                        return ins
                    P.op("pe", mm, reads=[RG("dwq", c)] + [RG("H", c_, t) for c_ in range(8)], writes=[("ps", bank)])
                    P.op("act", lambda e, t=t, bank=bank: e.activation(out=QA[0:64, 2 * c, tsl(t)], in_=PS[bank][0:64, :],
                                                                       func=AF.Copy, scale=0.125),
                         reads=[("ps", bank)], writes=[RG("Q", 2 * c, t)], extra=[tk])
                    P.op("dve", lambda e, t=t, bank=bank: e.tensor_scalar(out=QA[0:64, 2 * c + 1, tsl(t)], in0=PS[bank][64:128, :],
                                                                          scalar1=0.125, scalar2=None, op0=ALU.mult),
                         reads=[("ps", bank)], writes=[RG("Q", 2 * c + 1, t)], extra=[tk])

            conv_pe(0)
            conv_evac(0)
            conv_chain_act(0)
            conv_pe(1)
            conv_evac(1)
            for t in range(NT):
                conv_chain(t)
                if t + 2 < NT:
                    conv_pe(t + 2)
                    if t + 2 == NT - 1:
                        qstate["tok"] = ("pe", P.cnt["pe"])
                oth = []
                if t >= 2:
                    oth = q_proj_steps(2 * (t - 2)) + q_proj_steps(2 * (t - 2) + 1)
                conv_chain_b(t, oth)
                if t + 2 < NT:
                    conv_evac(t + 2)

            P.barrier()
            A.pos[0] = mark + 32768
            KA = A("mKA", [70, 8, S], BF16)
            p_vp = A.pos[0]
            VP = A("mVP", [128, 16, 512], BF16)
            p_w = A.pos[0]
            WB2 = [A("mWC%d" % i, [128, 1024], BF16) for i in range(4)]
            p_spare = A.pos[0]
            WVB = [nc.alloc_sbuf_tensor_at("mWV%d" % i, [128, 2048], BF16, offset=p_w + 4096 * i) for i in range(2)]
            SIGF = nc.alloc_sbuf_tensor_at("mSIGF", [8, S], F32, offset=p_vp)
            DD = nc.alloc_sbuf_tensor_at("mDD", [8, S], F32, offset=p_vp + 8192)
            D1 = nc.alloc_sbuf_tensor_at("mD1", [8, S], BF16, offset=p_vp)
            D2 = nc.alloc_sbuf_tensor_at("mD2", [8, S], BF16, offset=p_vp + 4096)


            for t in range(NT):
                bank = rotate("pa", [0, 1, 2, 3])

                def mm(e, t=t, bank=bank):
                    ins = None
                    for kc in range(8):
                        ins = e.matmul(PS[bank][0:8, :], WF[:, kc * 8:(kc + 1) * 8], H[:, kc, tsl(t)], start=(kc == 0), stop=(kc == 7))
                    return ins
                P.op("pe", mm, reads=[RG("WF")] + [RG("H", c, t) for c in range(8)], writes=[("ps", bank)])
                P.op("act", lambda e, t=t, bank=bank: e.activation(out=SIGF[0:8, tsl(t)], in_=PS[bank][0:8, :], func=AF.Sigmoid,
                                                                   bias=PRM[0:8, PC_FB:PC_FB + 1], scale=1.0),
                     reads=[("ps", bank), "PRM"], writes=[RG("SIGF", t)])
            P.op("act", lambda e: e.activation(out=SIGF[0:8, :], in_=SIGF[0:8, :], func=AF.Ln),
                 reads=[RG("SIGF", t) for t in range(NT)], writes=[RG("LOGF")])
            dtoks = []
            pieces = []
            for h_ in range(8):
                for t_ in range(NT):
                    pieces.append(lambda h_=h_, t_=t_: P.op("dve", lambda e: e.memset(QA[64:70, h_, tsl(t_)], -1.0),
                                                            writes=[RG("QAaug")]))
                    pieces.append(lambda h_=h_, t_=t_: P.op("pool", lambda e: e.memset(KA[64:70, h_, tsl(t_)], 1.0),
                                                            writes=[RG("KAaug")]))
            dq = [
                lambda: P.op("dve", lambda e: e.tensor_tensor_scan(out=DD[0:8, :], data0=SIGF[0:8, :], data1=SIGF[0:8, :],
                                                                   initial=0.0, op0=ALU.add, op1=ALU.min),
                             reads=[RG("LOGF")], writes=[RG("DD")]),
                lambda: P.op("dve", lambda e: e.tensor_copy(out=D1[0:8, :], in_=DD[0:8, :]), reads=[RG("DD")], writes=[RG("D1")]),
                lambda: P.op("dve", lambda e: e.tensor_tensor(out=DD[0:8, :], in0=DD[0:8, :], in1=D1[0:8, :], op=ALU.subtract),
                             reads=[RG("DD"), RG("D1")], writes=[RG("DD")]),
                lambda: P.op("dve", lambda e: e.tensor_copy(out=D2[0:8, :], in_=DD[0:8, :]), reads=[RG("DD")], writes=[RG("D2")]),
                lambda: P.op("dve", lambda e: e.tensor_tensor(out=DD[0:8, :], in0=DD[0:8, :], in1=D2[0:8, :], op=ALU.subtract),
                             reads=[RG("DD"), RG("D2")], writes=[RG("DD")]),
            ]

            def d12_dmas():
                while pieces:
                    pieces.pop(0)()
                for i, Di in enumerate((D1, D2)):
                    dtoks.append(P.dma("sp", lambda e, i=i, Di=Di: e.dma_start(out=QA[64 + i:65 + i, :, :], in_=Di[0:8, :]),
                                       ("d_dq", i), reads=[RG("D%d" % (i + 1)), RG("QAaug")], writes=[RG("QAd", i)]))
                    dtoks.append(P.dma("sp", lambda e, i=i, Di=Di: e.dma_start(out=KA[67 + i:68 + i, :, :], in_=Di[0:8, :]),
                                       ("d_dk", i), reads=[RG("D%d" % (i + 1)), RG("KAaug")], writes=[RG("KAd", i)]))
            dq.append(d12_dmas)
            AUGR = [RG("QAd", i) for i in range(3)] + [RG("KAd", i) for i in range(3)]

            def d3_dmas():
                dtoks.append(P.dma("pool", lambda e: e.dma_start(out=QA[66:67, :, :], in_=DD[0:8, :]), ("d_dq", 2),
                                   reads=[RG("DD"), RG("QAaug")], writes=[RG("QAd", 2)]))
                dtoks.append(P.dma("pool", lambda e: e.dma_start(out=KA[69:70, :, :], in_=DD[0:8, :]), ("d_dk", 2),
                                   reads=[RG("DD"), RG("KAaug")], writes=[RG("KAd", 2)]))

            wst["n"] = 0
            wst["slot"] = {}
            ktok = {}
            for qk in range(1, 2):
                for c in range(4):
                    ci = 8 + qk * 4 + c
                    load_win(ci, WB2, "dwc")
                    if c == 3:
                        P.dma("pool", lambda e: e.dma_start(out=WVB[0][:, :], in_=wvd[0]), ("d_wv", 0), writes=[RG("WV", 0)],
                              extra=[ktok[1]])
                    for t in range(NT):
                        bank = rotate("pa", [0, 1, 2, 3])
                        proj_chunk(ci, t, bank, WB2, "dwc")
                        dst = QA if qk == 0 else KA
                        scl = 0.125 if qk == 0 else 1.0
                        nmk = "Q" if qk == 0 else "K"
                        P.op("act", lambda e, dst=dst, c=c, t=t, bank=bank, scl=scl: e.activation(
                            out=dst[0:64, 2 * c, tsl(t)], in_=PS[bank][0:64, :], func=AF.Copy, scale=scl),
                            reads=[("ps", bank)], writes=[RG(nmk, 2 * c, t)])
                        P.op("dve", lambda e, dst=dst, c=c, t=t, bank=bank, scl=scl: e.tensor_scalar(
                            out=dst[0:64, 2 * c + 1, tsl(t)], in0=PS[bank][64:128, :], scalar1=scl, scalar2=None, op0=ALU.mult),
                            reads=[("ps", bank)], writes=[RG(nmk, 2 * c + 1, t)])
                        if t == NT - 1:
                            ktok[c] = ("pe", P.cnt["pe"])
                        for _ in range(4):
                            if pieces:
                                pieces.pop(0)()
                        if dq and (t % 2 == 1):
                            dq.pop(0)()
            while dq:
                dq.pop(0)()
            P.dma("pool", lambda e: e.dma_start(out=WVB[1][:, :], in_=wvd[1]), ("d_wv", 1), writes=[RG("WV", 1)],
                  extra=[ktok[3]])
            d3_dmas()
            for hf in range(2):
                for blk in range(16):
                    bank = rotate("pa", [0, 1, 2, 3])
                    t = blk // 4

                    def mm(e, blk=blk, bank=bank, hf=hf):
                        ins = None
                        for kc in range(8):
                            ins = e.matmul(PS[bank][:, 0:256], H[:, kc, blk * 128:(blk + 1) * 128],
                                           WVB[hf][:, kc * 256:(kc + 1) * 256], start=(kc == 0), stop=(kc == 7))
                        return ins
                    P.op("pe", mm, reads=[RG("WV", hf)] + [RG("H", c, t) for c in range(8)], writes=[("ps", bank)])
                    if blk % 2 == 0:
                        P.op("act", lambda e, blk=blk, bank=bank, hf=hf: e.activation(out=VP[:, blk, hf * 256:(hf + 1) * 256],
                                                                                      in_=PS[bank][:, 0:256], func=AF.Copy),
                             reads=[("ps", bank)], writes=[RG("VPh", blk, hf)], extra=dtoks)
                    else:
                        P.op("dve", lambda e, blk=blk, bank=bank, hf=hf: e.tensor_copy(out=VP[:, blk, hf * 256:(hf + 1) * 256],
                                                                                       in_=PS[bank][:, 0:256]),
                             reads=[("ps", bank)], writes=[RG("VPh", blk, hf)], extra=dtoks)
            tok_lastH = ("pe", P.cnt["pe"])

            A.pos[0] = M0
            WO = [A("mWO%d" % j, [128, 1024], BF16) for j in range(8)]
            YAT = A("mYAT", [128, 4, TS], F32)
            YNA = A("mYNA", [128, 4, TS], BF16)
            SSQ = A("mSSQ", [128, 4, TS], BF16)
            assert A.pos[0] <= M0 + 32768, A.pos[0] - M0
            A.pos[0] = p_w
            PT = [A("mPT%d" % i, [128, 2, TS], BF16) for i in range(3)]
            assert A.pos[0] <= p_w + 8192
            A.pos[0] = p_spare
            RR = A("mRR", [128, TS], F32)
            LN3 = MISC_T
            RS3 = MISC_T
            for j in range(8):
                P.dma("pool", lambda e, j=j: e.dma_start(out=WO[j][:, :], in_=wod[j]), ("d_wo", j), writes=[RG("WO", j)],
                      extra=[tok_lastH])

            pvbanks = {}
            for I in range(NT):
                for c in range(4):
                    pvbanks[(I, c)] = rotate("pvb", [(4, 5), (6, 7)])

            def unit(I, c, j):
                q0 = I * TS
                nkb = 4 * I + 4
                d = j - 4 * I
                n0 = 128 * d if d > 0 else 0
                k = rotate("sbp", [0, 1])
                sbanks = (2 * k, 2 * k + 1)
                pb = rotate("ptb", [0, 1, 2])
                bA, bB = pvbanks[(I, c)]
                last = (j == nkb - 1)

                def qk(e):
                    ins = None
                    for hh in range(2):
                        h = 2 * c + hh
                        sbk = sbanks[hh]
                        e.matmul(PS[sbk][0:64, n0:TS], KA[0:70, h, j * 128:j * 128 + 64], QA[0:70, h, q0 + n0:q0 + TS],
                                 start=True, stop=(d < 0))
                        ins = e.matmul(PS[sbk][64:128, n0:TS], KA[0:70, h, j * 128 + 64:(j + 1) * 128],
                                       QA[0:70, h, q0 + n0:q0 + TS], start=True, stop=(d < 0), tile_position=(0, 64))
                        if d >= 0:
                            e.matmul(PS[sbk][0:64, n0:n0 + 128], IDENT[:, 0:64], MASK[:, :], start=False, stop=True)
                            ins = e.matmul(PS[sbk][64:128, n0:n0 + 128], IDENT[:, 64:128], MASK[:, :], start=False, stop=True,
                                           tile_position=(0, 64))
                    return ins
                rd = ["IDENT", "MASK"] + AUGR
                for hh in range(2):
                    rd += [RG("K", 2 * c + hh, j // 4), RG("Q", 2 * c + hh, I)]
                P.op("pe", qk, reads=rd, writes=[("ps", sbanks[0]), ("ps", sbanks[1])])
                P.op("act", lambda e: e.activation(out=PT[pb][:, :, n0:TS], in_=PSB[k][:, :, n0:TS], func=AF.Exp),
                     reads=[("ps", sbanks[0]), ("ps", sbanks[1])], writes=[RG("PT", pb)])

                def pv():
                    def mm(e):
                        lo, hi = (slice(0, 64), slice(64, 128))
                        vA = VP[:, j, (2 * c) * 64:(2 * c + 1) * 64]
                        vB = VP[:, j, (2 * c + 1) * 64:(2 * c + 2) * 64]
                        e.matmul(PS[bA][lo, n0:TS], vA, PT[pb][:, 0, n0:TS], start=(j == 0), stop=last)
                        e.matmul(PS[bB][hi, n0:TS], ONE64[:, :], PT[pb][:, 0, n0:TS], start=(j == 0), stop=last,
                                 tile_position=(0, 64))
                        e.matmul(PS[bB][lo, n0:TS], ONE64[:, :], PT[pb][:, 1, n0:TS], start=(j == 0), stop=last)
                        return e.matmul(PS[bA][hi, n0:TS], vB, PT[pb][:, 1, n0:TS], start=(j == 0), stop=last,
                                        tile_position=(0, 64))
                    P.op("pe", mm, reads=[RG("PT", pb), RG("VPh", j, 0), RG("VPh", j, 1), RG("O64")], writes=[("ps", bA), ("ps", bB)])
                return pv

            def norm_pair(I, c):
                bA, bB = pvbanks[(I, c)]
                P.op("dve", lambda e: e.reciprocal(out=RR[:, :], in_=PS[bB][:, :]), reads=[("ps", bB)], writes=[RG("RR", 0), RG("RR", 1)])
                P.op("dve", lambda e: e.tensor_tensor(out=YAT[0:64, c, :], in0=PS[bA][0:64, :], in1=RR[64:128, :], op=ALU.mult),
                     reads=[("ps", bA), RG("RR", 1)], writes=[RG("YAT", c, 0)])
                P.op("dve", lambda e: e.tensor_tensor(out=YAT[64:128, c, :], in0=PS[bA][64:128, :], in1=RR[0:64, :], op=ALU.mult),
                     reads=[("ps", bA), RG("RR", 0)], writes=[RG("YAT", c, 1)])

            yr = [RG("YAT", c, k_) for c in range(4) for k_ in range(2)]

            def ssq(I):
                P.op("dve", lambda e: e.tensor_tensor(out=SSQ[:, :, :], in0=YAT[:, :, :], in1=YAT[:, :, :], op=ALU.mult),
                     reads=yr, writes=[RG("SSQ")])

            def epi2(I):
                bst = rotate("sb", [0, 1, 2, 3])
                stat_rstd(lambda c: SSQ[:, c, :], 4, [RG("SSQ")], bst, LN3, RS3, 1, RG("a3"))
                for c in range(4):
                    P.op("dve", lambda e, c=c: e.scalar_tensor_tensor(out=YNA[:, c, :], in0=YAT[:, c, :],
                                                                     scalar=GS[:, 36 + c:37 + c], in1=RS3[:, :],
                                                                     op0=ALU.mult, op1=ALU.mult),
                         reads=[RG("YAT", c, 0), RG("YAT", c, 1), (RG("a3"), "rs")] + GSR, writes=[RG("YNA", c)])
                if stop_after == "mixer_y":
                    P.dma("pool", lambda e: e.dma_start(out=outTv[:, 4:8, tsl(I)], in_=YNA[:, :, :]), ("d_out", 2),
                          reads=[RG("YNA", c) for c in range(4)])
                    P.dma("pool", lambda e: e.dma_start(out=outTv[:, 0:4, tsl(I)], in_=YNC[:, :, tsl(I)]), ("d_out", 3),
                          reads=[RG("YNC", c, I) for c in range(4)])

            def wout(I):
                for jo in range(8):
                    bo = rotate("sb", [0, 1, 2, 3])

                    def mm(e, jo=jo, bo=bo):
                        ins = None
                        for kc in range(8):
                            rhs = YNC[:, kc, tsl(I)] if kc < 4 else YNA[:, kc - 4, :]
                            ins = e.matmul(PS[bo][:, :], WO[jo][:, kc * 128:(kc + 1) * 128], rhs, start=(kc == 0), stop=(kc == 7))
                        return ins
                    P.op("pe", mm, reads=[RG("WO", jo)] + [RG("YNC", c, I) for c in range(4)] + [RG("YNA", c) for c in range(4)],
                         writes=[("ps", bo)])
                    P.op("dve", lambda e, jo=jo, bo=bo: e.tensor_tensor(out=X[:, jo, tsl(I)], in0=PS[bo][:, :], in1=X[:, jo, tsl(I)],
                                                                      op=ALU.add),
                         reads=[("ps", bo), ("X", jo, I)], writes=[("X", jo, I)])

            items = []
            for I in range(NT):
                for c in range(4):
                    for j in range(4 * I + 4):
                        items.append(("unit", I, c, j))
                    hooks = []
                    if c == 0 and I > 0:
                        hooks.append(lambda I=I: epi2(I - 1))
                    hooks.append(lambda I=I, c=c: norm_pair(I, c))
                    if c == 3:
                        hooks.append(lambda I=I: ssq(I))
                    if c == 0 and I > 0:
                        hooks.append(lambda I=I: wout(I - 1))
                    if c == 3 and I == NT - 1:
                        hooks.append(lambda I=I: epi2(I))
                        hooks.append(lambda I=I: wout(I))
                    items.append(("hooks", hooks))
            pendq = []
            for it in items:
                if it[0] == "unit":
                    pvf = unit(it[1], it[2], it[3])
                    pendq.append([pvf, []])
                    if len(pendq) > 2:
                        old = pendq.pop(0)
                        old[0]()
                        for hk in old[1]:
                            hk()
                else:
                    pendq[-1][1].extend(it[1])
            while pendq:
                old = pendq.pop(0)
                old[0]()
                for hk in old[1]:
                    hk()

        def final_stage():
            P.barrier()
            A = stage_alloc()
            SQ = [A("fSQ%d" % i, [128, 8, TS], BF16) for i in range(2)]
            LNT = [A("fLNT%d" % i, [128, TS], F32) for i in range(2)]
            RS = [A("fRS%d" % i, [128, TS], F32) for i in range(2)]
            OT = [A("fOT%d" % i, [128, 8, TS], F32) for i in range(2)]
            for t in range(NT):
                b = t % 2
                x_norm(t, PC_GF, SQ, LNT, RS, lambda c, b=b: OT[b][:, c, :], lambda c, b=b: ("OT", b, c), [6, 7])
                P.dma("sp", lambda e, t=t, b=b: e.dma_start(out=outTv[:, :, tsl(t)], in_=OT[b][:, :, :]), ("d_out", b),
                      reads=[("OT", b, c) for c in range(8)])

        def dump_stage():
            P.barrier()
            for t in range(NT):
                P.dma("sp", lambda e, t=t: e.dma_start(out=outTv[:, :, tsl(t)], in_=X[:, :, tsl(t)]), ("d_out", t % 2),
                      reads=[("X", c, t) for c in range(8)])

        hbox = {}
        stages = [("ffn1", lambda: hbox.__setitem__("H", ffn_stage(0))), ("mixer", lambda: mixer_stage(hbox["H"])),
                  ("ffn2", lambda: ffn_stage(1))]
        stopped = False
        for sname, sfn in stages:
            sfn()
            if stop_after == "mixer_y" and sname == "mixer":
                stopped = True
                break
            if stop_after == sname:
                dump_stage()
                stopped = True
                break
        if stop_after == "ffn1":
            raise RuntimeError("ffn1 dump unsupported with fused norms")
        finals = [(k, v) for k, v in P.cnt.items() if isinstance(k, tuple) and k[0] == "d_out"]
        P.run(finals)
    return nc


_CACHE = {}


def _prep_w13(w13):
    g = w13[:, :DFF].reshape(8, 128, NF, 128)
    u = w13[:, DFF:].reshape(8, 128, NF, 128)
    gu = np.concatenate([g, u], axis=3)
    return np.ascontiguousarray(gu.transpose(2, 1, 0, 3)).reshape(NF, 128, 2048)


def _prep_cols(w, ncol):
    a = w.reshape(8, 128, ncol, 128)
    return np.ascontiguousarray(a.transpose(2, 1, 0, 3)).reshape(ncol, 128, 1024)


def kernel(x, ffn1_norm, ffn1_w13, ffn1_w2, mix_norm, w_in, conv_w, conv_b, conv_ln_g, conv_ln_b, forget_b,
           out_norm_conv, out_norm_attn, w_out, ffn2_norm, ffn2_w13, ffn2_w2, final_norm):
    f32 = np.float32
    x = np.asarray(x, f32)
    B = x.shape[0]

    def vec8(v):
        return np.asarray(v, f32).reshape(8, 128).T

    def vec4(v):
        return np.asarray(v, f32).reshape(4, 128).T

    prm = np.zeros((128, NPRM), f32)
    prm[:, PC_G1:PC_G1 + 8] = vec8(ffn1_norm[0])
    prm[:, PC_GM:PC_GM + 8] = vec8(mix_norm[0])
    prm[:, PC_G2:PC_G2 + 8] = vec8(ffn2_norm[0])
    prm[:, PC_GF:PC_GF + 8] = vec8(final_norm)
    prm[:, PC_CB:PC_CB + 4] = vec4(conv_b[0])
    prm[:, PC_LG:PC_LG + 4] = vec4(conv_ln_g[0])
    prm[:, PC_LB:PC_LB + 4] = vec4(conv_ln_b[0])
    prm[:, PC_ONC:PC_ONC + 4] = vec4(out_norm_conv[0])
    prm[:, PC_ONA:PC_ONA + 4] = vec4(out_norm_attn[0])
    cw = np.asarray(conv_w[0], f32)
    prm[:, PC_CW:PC_CW + 4 * CW] = cw.reshape(CW, 4, 128).transpose(2, 1, 0).reshape(128, 4 * CW)
    prm[0:8, PC_FB] = np.asarray(forget_b[0], f32)

    win = np.asarray(w_in[0], f32)
    shared = {
        "prm": prm,
        "w13a": _prep_w13(np.asarray(ffn1_w13[0], f32)),
        "w2a": np.ascontiguousarray(np.asarray(ffn1_w2[0], f32).reshape(NF, 128, 1024)),
        "w13b": _prep_w13(np.asarray(ffn2_w13[0], f32)),
        "w2b": np.ascontiguousarray(np.asarray(ffn2_w2[0], f32).reshape(NF, 128, 1024)),
        "win": _prep_cols(win[:, 0:2048], 16),
        "wv": np.ascontiguousarray(win[:, 2048:2560].reshape(8, 128, 2, 256).transpose(2, 1, 0, 3)).reshape(2, 128, 2048),
        "wf": np.ascontiguousarray(win[:, 2560:2568].reshape(8, 128, 8).transpose(1, 0, 2)).reshape(128, 64),
        "wo": _prep_cols(np.asarray(w_out[0], f32), 8),
    }
    if "nc" not in _CACHE:
        _CACHE["nc"] = build_program()
    nc = _CACHE["nc"]
    in_maps = []
    for b in range(B):
        m = dict(shared)
        m["xT"] = np.ascontiguousarray(x[b].T)
        in_maps.append(m)
    res = run_bass_kernel_spmd(nc, in_maps, core_ids=list(range(B)))
    out = np.empty((B, S, D), f32)
    for b in range(B):
        out[b] = np.asarray(res.results[b]["outT"]).T
    return out
```
